# Optimizing a Trainium2 kernel written in Bass

```python
import jax, jax.numpy as jnp
from jax import lax
import numpy as np

D_MODEL = 1024
BATCH = 2
SEQ = 16384
DEPTH = 2

GRID_W = 64
CTX_LEN = 256
N_HEADS = 4
D_K = D_MODEL // 2 // N_HEADS
D_V = D_MODEL // N_HEADS
QK_W = N_HEADS * D_K
V_W = N_HEADS * D_V
CHUNK = 64
GLA_RANK = 16
GLA_TAU = 16.0
SHORT_CONV = 3
D_FF = 2816
N_MOD = 6
EPS = 1e-6
N_GLA = (DEPTH + 1) // 2
N_MLSTM = DEPTH // 2
GLA_IN = 2 * QK_W + 2 * V_W + 2 * GLA_RANK
MLSTM_IN = 2 * QK_W + 2 * V_W + 4 * N_HEADS
SPLITS = [QK_W, 2 * QK_W, 2 * QK_W + V_W, 2 * QK_W + 2 * V_W]

kernel_name = 'hybrid_gla_mlstm_convffn_prefix_ctx'


def rmsnorm(x, g):
    x32 = x.astype(jnp.float32)
    y = x32 * lax.rsqrt(jnp.mean(x32 * x32, axis=-1, keepdims=True) + EPS)
    return (y * g.astype(jnp.float32)).astype(x.dtype)


def head_rmsnorm(o, g):
    o32 = o.astype(jnp.float32)
    y = o32 * lax.rsqrt(jnp.mean(o32 * o32, axis=-1, keepdims=True) + EPS)
    bsz, nh, t_len, dv = o.shape
    y = y.transpose(0, 2, 1, 3).reshape(bsz, t_len, nh * dv)
    return (y * g.astype(jnp.float32)).astype(o.dtype)


def to_heads(t, d):
    bsz, t_len, _ = t.shape
    return t.reshape(bsz, t_len, N_HEADS, d).transpose(0, 2, 1, 3)


def conv1d_centred(x, w, b):
    xp = jnp.pad(x, ((0, 0), (1, 1), (0, 0)))
    return xp[:, :-2] * w[0] + xp[:, 1:-1] * w[1] + xp[:, 2:] * w[2] + b


def conv2d_grid(x, w, b, rows):
    bsz, t_len, ch = x.shape
    xg = x.reshape(bsz, rows, GRID_W, ch)
    y = lax.conv_general_dilated(xg, w[:, :, None, :].astype(x.dtype), (1, 1), 'SAME',
                                 dimension_numbers=('NHWC', 'HWIO', 'NHWC'),
                                 feature_group_count=ch)
    return y.reshape(bsz, t_len, ch) + b


def to_colmajor(x, rows):
    bsz, t_len, d = x.shape
    return x.reshape(bsz, rows, GRID_W, d).transpose(0, 2, 1, 3).reshape(bsz, t_len, d)


def from_colmajor(x, rows):
    bsz, t_len, d = x.shape
    return x.reshape(bsz, GRID_W, rows, d).transpose(0, 2, 1, 3).reshape(bsz, t_len, d)


def gla_scan(q, k, v, log_a, s0, reverse):
    dt = v.dtype
    q, k, v, log_a = (t.astype(jnp.float32) for t in (q, k, v, log_a))
    if reverse:
        q, k, v, log_a = (jnp.flip(t, axis=2) for t in (q, k, v, log_a))
    bsz, nh, t_len, dk = q.shape
    dv = v.shape[-1]
    n = t_len // CHUNK
    q, k, v, log_a = (t.reshape(bsz, nh, n, CHUNK, t.shape[-1]) for t in (q, k, v, log_a))
    b = jnp.cumsum(log_a, axis=3)
    b_last = b[:, :, :, -1:, :]
    q_in = q * jnp.exp(b)
    tril = jnp.tril(jnp.ones((CHUNK, CHUNK), dtype=bool))
    scores = jnp.einsum('bhnid,bhnjd->bhnij', q_in, k * jnp.exp(-b))
    scores = jnp.where(tril, scores, 0.0)
    o_intra = jnp.einsum('bhnij,bhnjv->bhniv', scores, v)
    k_st = k * jnp.exp(b_last - b)
    decay = jnp.exp(b_last[:, :, :, 0, :])

    def step(s, inp):
        qc, kc, vc, dc = inp
        o = jnp.einsum('bhcd,bhdv->bhcv', qc, s)
        s = s * dc[..., None] + jnp.einsum('bhcd,bhcv->bhdv', kc, vc)
        return s, o

    xs = tuple(jnp.moveaxis(t, 2, 0) for t in (q_in, k_st, v, decay))
    s_fin, o_inter = lax.scan(step, s0, xs)
    o = (o_intra + jnp.moveaxis(o_inter, 0, 2)).reshape(bsz, nh, t_len, dv)
    if reverse:
        o = jnp.flip(o, axis=2)
    return o.astype(dt), s_fin


def mlstm_scan(q, k, v, i_pre, logf, state, reverse):
    dt = v.dtype
    q, k, v, i_pre, logf = (t.astype(jnp.float32) for t in (q, k, v, i_pre, logf))
    if reverse:
        q, k, v, i_pre, logf = (jnp.flip(t, axis=2) for t in (q, k, v, i_pre, logf))
    bsz, nh, t_len, dk = q.shape
    dv = v.shape[-1]
    n = t_len // CHUNK
    q, k, v = (jnp.moveaxis(t.reshape(bsz, nh, n, CHUNK, t.shape[-1]), 2, 0) for t in (q, k, v))
    i_c = jnp.moveaxis(i_pre.reshape(bsz, nh, n, CHUNK), 2, 0)
    f_cum = jnp.cumsum(jnp.moveaxis(logf.reshape(bsz, nh, n, CHUNK), 2, 0), axis=-1)
    tril = jnp.tril(jnp.ones((CHUNK, CHUNK), dtype=bool))

    def step(carry, inp):
        c_st, n_st, m = carry
        qc, kc, vc, ic, fc = inp
        d_mat = fc[..., :, None] - fc[..., None, :] + ic[..., None, :]
        d_mat = jnp.where(tril, d_mat, -jnp.inf)
        inter = fc + m[..., None]
        m_i = jnp.maximum(inter, jnp.max(d_mat, axis=-1))
        w_inter = jnp.exp(inter - m_i)
        s_mat = jnp.einsum('bhid,bhjd->bhij', qc, kc) * jnp.exp(d_mat - m_i[..., None])
        num = w_inter[..., None] * jnp.einsum('bhid,bhdv->bhiv', qc, c_st) + jnp.einsum('bhij,bhjv->bhiv', s_mat, vc)
        den = w_inter * jnp.einsum('bhid,bhd->bhi', qc, n_st) + jnp.sum(s_mat, axis=-1)
        h = num / jnp.maximum(jnp.abs(den), jnp.exp(-m_i))[..., None]
        f_last = fc[..., -1]
        g = f_last[..., None] - fc + ic
        m_new = jnp.maximum(f_last + m, jnp.max(g, axis=-1))
        dec = jnp.exp(f_last + m - m_new)
        wk = jnp.exp(g - m_new[..., None])
        c_st = dec[..., None, None] * c_st + jnp.einsum('bhj,bhjd,bhjv->bhdv', wk, kc, vc)
        n_st = dec[..., None] * n_st + jnp.einsum('bhj,bhjd->bhd', wk, kc)
        return (c_st, n_st, m_new), h

    state_fin, h = lax.scan(step, state, (q, k, v, i_c, f_cum))
    h = jnp.moveaxis(h, 0, 2).reshape(bsz, nh, t_len, dv)
    if reverse:
        h = jnp.flip(h, axis=2)
    return h.astype(dt), state_fin


def gla_project(h, w_in, w_a2, b_a2):
    q, k, v, r, za = jnp.split(h @ w_in, SPLITS, axis=-1)
    log_a = [to_heads(jax.nn.log_sigmoid((za[..., d * GLA_RANK:(d + 1) * GLA_RANK] @ w_a2[d] + b_a2[d]).astype(jnp.float32)) / GLA_TAU, D_K)
             for d in range(2)]
    return to_heads(q, D_K) * D_K ** -0.5, to_heads(k, D_K), to_heads(v, D_V), r, log_a


def gla_mixer(h_ctx, h_lat, need_ctx, w_in, w_a2, b_a2, norm_g, w_out):
    qc, kc, vc, rc, lac = gla_project(h_ctx, w_in, w_a2, b_a2)
    ql, kl, vl, rl, lal = gla_project(h_lat, w_in, w_a2, b_a2)
    bsz = h_lat.shape[0]
    s0 = jnp.zeros((bsz, N_HEADS, D_K, D_V), jnp.float32)
    o_ctx, o_lat = 0.0, 0.0
    for d, rev in enumerate((False, True)):
        oc, s_c = gla_scan(qc, kc, vc, lac[d], s0, rev)
        ol, _ = gla_scan(ql, kl, vl, lal[d], s_c, rev)
        o_ctx, o_lat = o_ctx + oc, o_lat + ol

    def finish(o, r):
        return (head_rmsnorm(o, norm_g) * jax.nn.silu(r)) @ w_out

    return (finish(o_ctx, rc) if need_ctx else None), finish(o_lat, rl)


def mlstm_project(h, w_in, b_gate, conv_w, conv_b):
    q, k, v, o, gates = jnp.split(h @ w_in, SPLITS, axis=-1)
    qk = jax.nn.silu(conv1d_centred(jnp.concatenate([q, k], axis=-1), conv_w, conv_b))
    q, k = jnp.split(qk, 2, axis=-1)
    bsz, t_len, _ = h.shape
    gates = (gates + b_gate).astype(jnp.float32).reshape(bsz, t_len, 4, N_HEADS).transpose(2, 0, 3, 1)
    i_pre = gates[0:2]
    logf = jax.nn.log_sigmoid(gates[2:4])
    return to_heads(q, D_K), to_heads(k, D_K) * D_K ** -0.5, to_heads(v, D_V), o, i_pre, logf


def mlstm_mixer(h_ctx, h_lat, rows, need_ctx, w_in, b_gate, conv_w, conv_b, norm_g, w_out):
    h_lat = to_colmajor(h_lat, rows)
    qc, kc, vc, oc_g, ic, fc = mlstm_project(h_ctx, w_in, b_gate, conv_w, conv_b)
    ql, kl, vl, ol_g, il, fl = mlstm_project(h_lat, w_in, b_gate, conv_w, conv_b)
    bsz = h_lat.shape[0]
    st0 = (jnp.zeros((bsz, N_HEADS, D_K, D_V), jnp.float32),
           jnp.zeros((bsz, N_HEADS, D_K), jnp.float32),
           jnp.zeros((bsz, N_HEADS), jnp.float32))
    h_ctx_sum, h_lat_sum = 0.0, 0.0
    for d, rev in enumerate((False, True)):
        hc, st_c = mlstm_scan(qc, kc, vc, ic[d], fc[d], st0, rev)
        hl, _ = mlstm_scan(ql, kl, vl, il[d], fl[d], st_c, rev)
        h_ctx_sum, h_lat_sum = h_ctx_sum + hc, h_lat_sum + hl

    def finish(hsum, o_g):
        return (head_rmsnorm(hsum, norm_g) * jax.nn.sigmoid(o_g)) @ w_out

    out_lat = from_colmajor(finish(h_lat_sum, ol_g), rows)
    return (finish(h_ctx_sum, oc_g) if need_ctx else None), out_lat


def conv_ffn(h, w_up, conv_w, conv_b, w_down, rows):
    a, g = jnp.split(h @ w_up, 2, axis=-1)
    if rows is None:
        g = conv1d_centred(g, conv_w[1], conv_b)
    else:
        g = conv2d_grid(g, conv_w, conv_b, rows)
    return (jax.nn.gelu(g) * a) @ w_down


def setup_inputs(seed: int = 0) -> dict:
    key = jax.random.key(seed)
    ks = jax.random.split(key, 26)
    D = D_MODEL

    def nrm(k, shape, scale):
        return jax.random.normal(k, shape, jnp.float32) * scale

    return {
        'x': nrm(ks[0], (BATCH, SEQ, D), 1.0),
        'c': nrm(ks[1], (BATCH, D), 1.0),
        'ctx': nrm(ks[2], (BATCH, CTX_LEN, D), 1.0),
        'c_ctx': nrm(ks[3], (D,), 1.0),
        'ada_w': nrm(ks[4], (DEPTH, D, N_MOD * D), 0.5 * D ** -0.5),
        'ada_b': nrm(ks[5], (DEPTH, N_MOD * D), 0.02),
        'norm_mix_g': 1.0 + nrm(ks[6], (DEPTH, D), 0.02),
        'norm_ffn_g': 1.0 + nrm(ks[7], (DEPTH, D), 0.02),
        'gla_w_in': nrm(ks[8], (N_GLA, D, GLA_IN), D ** -0.5),
        'gla_w_a2': nrm(ks[9], (N_GLA, 2, GLA_RANK, QK_W), GLA_RANK ** -0.5),
        'gla_b_a2': nrm(ks[10], (N_GLA, 2, QK_W), 0.1),
        'gla_norm_g': 1.0 + nrm(ks[11], (N_GLA, V_W), 0.02),
        'gla_w_out': nrm(ks[12], (N_GLA, V_W, D), V_W ** -0.5),
        'mlstm_w_in': nrm(ks[13], (N_MLSTM, D, MLSTM_IN), D ** -0.5),
        'mlstm_b_gate': jnp.concatenate([nrm(ks[14], (N_MLSTM, 2 * N_HEADS), 0.1),
                                         3.0 + nrm(ks[15], (N_MLSTM, 2 * N_HEADS), 0.5)], axis=-1),
        'mlstm_conv_w': nrm(ks[16], (N_MLSTM, SHORT_CONV, 2 * QK_W), SHORT_CONV ** -0.5),
        'mlstm_conv_b': nrm(ks[17], (N_MLSTM, 2 * QK_W), 0.02),
        'mlstm_norm_g': 1.0 + nrm(ks[18], (N_MLSTM, V_W), 0.02),
        'mlstm_w_out': nrm(ks[19], (N_MLSTM, V_W, D), V_W ** -0.5),
        'ffn_w_up': nrm(ks[20], (DEPTH, D, 2 * D_FF), D ** -0.5),
        'ffn_conv_w': nrm(ks[21], (DEPTH, 3, 3, D_FF), 1.0 / 3.0),
        'ffn_conv_b': nrm(ks[22], (DEPTH, D_FF), 0.02),
        'ffn_w_down': nrm(ks[23], (DEPTH, D_FF, D), D_FF ** -0.5),
        'final_norm_g': 1.0 + nrm(ks[24], (D,), 0.02),
    }


def reference(x, c, ctx, c_ctx, ada_w, ada_b, norm_mix_g, norm_ffn_g,
              gla_w_in, gla_w_a2, gla_b_a2, gla_norm_g, gla_w_out,
              mlstm_w_in, mlstm_b_gate, mlstm_conv_w, mlstm_conv_b, mlstm_norm_g, mlstm_w_out,
              ffn_w_up, ffn_conv_w, ffn_conv_b, ffn_w_down, final_norm_g):
    rows = x.shape[1] // GRID_W
    for i in range(DEPTH):
        need_ctx = i < DEPTH - 1
        j = i // 2
        mod_l = (jax.nn.silu(c) @ ada_w[i] + ada_b[i])[:, None, :]
        mod_c = jax.nn.silu(c_ctx) @ ada_w[i] + ada_b[i]
        sh1_l, sc1_l, g1_l, sh2_l, sc2_l, g2_l = jnp.split(mod_l, N_MOD, axis=-1)
        sh1_c, sc1_c, g1_c, sh2_c, sc2_c, g2_c = jnp.split(mod_c, N_MOD, axis=-1)
        h_c = rmsnorm(ctx, norm_mix_g[i]) * (1.0 + sc1_c) + sh1_c
        h_l = rmsnorm(x, norm_mix_g[i]) * (1.0 + sc1_l) + sh1_l
        if i % 2 == 0:
            o_c, o_l = gla_mixer(h_c, h_l, need_ctx, gla_w_in[j], gla_w_a2[j], gla_b_a2[j],
                                 gla_norm_g[j], gla_w_out[j])
        else:
            o_c, o_l = mlstm_mixer(h_c, h_l, rows, need_ctx, mlstm_w_in[j], mlstm_b_gate[j],
                                   mlstm_conv_w[j], mlstm_conv_b[j], mlstm_norm_g[j], mlstm_w_out[j])
        x = x + g1_l * o_l
        h_l = rmsnorm(x, norm_ffn_g[i]) * (1.0 + sc2_l) + sh2_l
        x = x + g2_l * conv_ffn(h_l, ffn_w_up[i], ffn_conv_w[i], ffn_conv_b[i], ffn_w_down[i], rows)
        if need_ctx:
            ctx = ctx + g1_c * o_c
            h_c = rmsnorm(ctx, norm_ffn_g[i]) * (1.0 + sc2_c) + sh2_c
            ctx = ctx + g2_c * conv_ffn(h_c, ffn_w_up[i], ffn_conv_w[i], ffn_conv_b[i], ffn_w_down[i], None)
    return rmsnorm(x, final_norm_g)
```

```python
import os
from contextlib import ExitStack
CUT = 9


import numpy as np
import ml_dtypes
import concourse.bass as bass
import concourse.mybir as mybir
from concourse.bass_utils import run_bass_kernel_spmd

F32 = mybir.dt.float32
BF16 = mybir.dt.bfloat16
AF = mybir.ActivationFunctionType
ALU = mybir.AluOpType
AX = mybir.AxisListType


class Prog:
    CE = ("pe", "act", "dve", "pool")

    def __init__(self, nc, ndma=24, same_engine_sync=True):
        self.nc = nc
        self.q = {e: [] for e in ("pe", "act", "dve", "pool", "sp")}
        self.cnt = {e: 0 for e in self.CE}
        self.sem = {e: nc.alloc_semaphore("s_" + e) for e in self.CE}
        self.dsem = [nc.alloc_semaphore("s_dma%d" % i) for i in range(2 * ndma)]
        self.dcum = [0] * (2 * ndma)
        self.ndma = ndma
        self.dnext = {"sp": 0, "act": 0, "pool": 0}
        self.waited = {e: {} for e in self.q}
        self.lastw = {}
        self.readers = {}
        self.ses = same_engine_sync
        self.nwaits = 0

    def _semh(self, sk):
        return self.sem[sk] if isinstance(sk, str) else self.dsem[sk[1]]

    def _wait(self, eng, sk, val):
        if sk == eng and (eng == "pe" or not self.ses):
            return
        if self.waited[eng].get(sk, 0) >= val:
            return
        self.waited[eng][sk] = val
        h = self._semh(sk)
        self.nwaits += 1
        self.q[eng].append(lambda E, h=h, val=val: E.wait_ge(h, val))

    def _deps(self, eng, reads, writes):
        for k in reads:
            if k in self.lastw:
                self._wait(eng, *self.lastw[k])
        for k in writes:
            if k in self.lastw:
                self._wait(eng, *self.lastw[k])
            for sk, val in self.readers.get(k, {}).items():
                self._wait(eng, sk, val)

    def _record(self, ev, reads, writes):
        for k in writes:
            self.lastw[k] = ev
            self.readers[k] = {}
        for k in reads:
            r = self.readers.setdefault(k, {})
            if r.get(ev[0], 0) < ev[1]:
                r[ev[0]] = ev[1]

    def op(self, eng, fn, reads=(), writes=()):
        self._deps(eng, reads, writes)
        self.cnt[eng] += 1
        ev = (eng, self.cnt[eng])
        h = self.sem[eng]
        self.q[eng].append(lambda E, fn=fn, h=h: fn(E).then_inc(h, 1))
        if self.ses or eng == "pe":
            pass
        self._record(ev, reads, writes)
        return ev

    def dma(self, qeng, out, in_, reads=(), writes=(), **kw):
        self._deps(qeng, reads, writes)
        i = self.dnext[qeng] + (self.ndma if qeng == "pool" else 0)
        self.dnext[qeng] = (self.dnext[qeng] + 1) % self.ndma
        if self.dcum[i] > 0:
            self._wait(qeng, ("dma", i), self.dcum[i])
        self.dcum[i] += 16
        ev = (("dma", i), self.dcum[i])
        h = self.dsem[i]
        self.q[qeng].append(lambda E, out=out, in_=in_, h=h, kw=kw: E.dma_start(out=out, in_=in_, **kw).then_inc(h, 16))
        self._record(ev, reads, writes)
        return ev

    def barrier(self):
        for e in self.q:
            for i, c in enumerate(self.dcum):
                if c > 0:
                    self._wait(e, ("dma", i), c)
            for e2 in self.CE:
                if e2 != e and self.cnt[e2] > 0:
                    self._wait(e, e2, self.cnt[e2])
        self.lastw = {}
        self.readers = {}

    def finish(self):
        for i, c in enumerate(self.dcum):
            if c > 0:
                self._wait("sp", ("dma", i), c)
        for e in self.CE:
            if self.cnt[e] > 0:
                self._wait("sp", e, self.cnt[e])
        nc = self.nc
        with nc.Block() as block:
            @block.sync
            def _(E):
                for f in self.q["sp"]:
                    f(E)
            @block.tensor
            def _(E):
                for f in self.q["pe"]:
                    f(E)
            @block.scalar
            def _(E):
                for f in self.q["act"]:
                    f(E)
            @block.vector
            def _(E):
                for f in self.q["dve"]:
                    f(E)
            @block.gpsimd
            def _(E):
                for f in self.q["pool"]:
                    f(E)


T_ALL = 16640
NTILE = T_ALL // 128
D = 1024
WCOLS = 832


def p0_consts():
    ident = np.eye(128, dtype=np.float32).astype(ml_dtypes.bfloat16)
    s = np.arange(128)[:, None]
    t = np.arange(128)[None, :]
    same = (s // 64) == (t // 64)
    m = np.zeros((128, 8, 128), np.float32)
    m[:, 0] = np.where(same & (s <= t), -1.0 / 16, 0)
    m[:, 1] = np.where(same & (s >= t), -1.0 / 16, 0)
    m[:, 2] = np.where(same & (s > t), -1.0 / 16, 0)
    m[:, 3] = np.where(same & (s < t), -1.0 / 16, 0)
    m[:, 4] = np.where(same & (t >= s), 1.0, 0)
    m[:, 5] = np.where(same & (t <= s), 1.0, 0)
    return ident, m.astype(ml_dtypes.bfloat16), m[:, 4:6].copy()


def p0_inputs(inp, core):
    b, h = core // 4, core % 4
    f = np.float32
    xall = np.concatenate([inp["ctx"][b], inp["x"][b]], axis=0)
    cT = np.stack([inp["c"][b], inp["c_ctx"]], axis=-1).reshape(8, 128, 2).transpose(1, 0, 2)
    adaw = inp["ada_w"][0][:, 0:2048]
    adabT = inp["ada_b"][0][0:2048].reshape(16, 128).T
    gmixT = inp["norm_mix_g"][0].reshape(8, 128).T
    w = inp["gla_w_in"][0]
    win = np.zeros((1024, WCOLS), f)
    win[:, 0:128] = w[:, h * 128:(h + 1) * 128]
    win[:, 128:256] = w[:, 512 + h * 128:512 + (h + 1) * 128]
    win[:, 256:512] = w[:, 1024 + h * 256:1024 + (h + 1) * 256]
    win[:, 512:768] = w[:, 2048 + h * 256:2048 + (h + 1) * 256]
    win[:, 768:784] = w[:, 3072:3088]
    win[:, 800:816] = w[:, 3088:3104]
    wa2 = np.zeros((64, 256), f)
    for d in range(2):
        wa2[32 * d:32 * d + 16, 128 * d:128 * (d + 1)] = inp["gla_w_a2"][0, d][:, h * 128:(h + 1) * 128]
    ba2 = np.ascontiguousarray(np.broadcast_to(np.concatenate([inp["gla_b_a2"][0, d][h * 128:(h + 1) * 128] for d in range(2)])[None, :], (128, 256)))
    gn = np.broadcast_to(inp["gla_norm_g"][0][h * 256:(h + 1) * 256][None, :], (128, 256))
    ident, masks, smask = p0_consts()
    return {"xall": np.ascontiguousarray(xall), "cT": np.ascontiguousarray(cT), "adaw": np.ascontiguousarray(adaw),
            "adabT": np.ascontiguousarray(adabT), "gmixT": np.ascontiguousarray(gmixT), "win": win, "wa2": wa2, "ba2": ba2,
            "gn": np.ascontiguousarray(gn), "ident": ident, "masks": masks}


def modvec(nc, p, es, cT, adaw, adabT, nchunk16, pm_name="pmod"):
    NM = nchunk16
    SBm = lambda name, shape, dt: es.enter_context(nc.sbuf_tensor(name, shape, dt))
    sc = SBm("mv_sc", [128, 8, 2], F32)
    sc2 = SBm("mv_sc2", [128, 8, 2], F32)
    ab = SBm("mv_ab", [128, NM], F32)
    mod = SBm("mv_mod", [128, NM, 2], F32)
    aw = [SBm("mv_aw%d" % i, [128, 8, 512], F32) for i in range(2)]
    pm = es.enter_context(nc.psum_tensor(pm_name, [128, NM, 2], F32))
    p.dma("sp", sc[:], cT, writes=["mv_sc"])
    p.dma("sp", ab[:], adabT, writes=["mv_ab"])
    p.op("act", lambda E: E.activation(out=sc2[:], in_=sc[:], func=AF.Exp, scale=-1.0), reads=["mv_sc"], writes=["mv_sc2"])
    p.op("dve", lambda E: E.tensor_scalar_add(out=sc2[:], in0=sc2[:], scalar1=1.0), reads=["mv_sc2"], writes=["mv_sc2"])
    p.op("dve", lambda E: E.reciprocal(out=sc2[:], in_=sc2[:]), reads=["mv_sc2"], writes=["mv_sc2"])
    p.op("dve", lambda E: E.tensor_mul(out=sc[:], in0=sc[:], in1=sc2[:]), reads=["mv_sc2", "mv_sc"], writes=["mv_sc"])
    npiece = NM // 4
    adv = adaw.rearrange("(j q) n -> q j n", q=128)
    for pc in range(npiece):
        s = pc % 2
        p.dma("sp", aw[s][:], adv[:, :, pc * 512:(pc + 1) * 512], writes=[("mv_aw", s)])
        for mm in range(4):
            m = pc * 4 + mm
            for j in range(8):
                p.op("pe", lambda E, s=s, mm=mm, m=m, j=j: E.matmul(pm[:, m, :], lhsT=aw[s][:, j, mm * 128:(mm + 1) * 128], rhs=sc[:, j, :],
                                                                    start=(j == 0), stop=(j == 7)),
                     reads=[("mv_aw", s), "mv_sc"], writes=["pmod"])
    for r in range(2):
        p.op("dve", lambda E, r=r: E.tensor_tensor(out=mod[:, :, r], in0=pm[:, :, r], in1=ab[:], op=ALU.add),
             reads=["pmod", "mv_ab"], writes=["mod"])
    return mod


def build_p0(nc, p, xall, cT, adaw, adabT, gmixT, win, wa2, ba2, gn, ident, masks, yT, ngroups=None, stage='all'):
    feat = nc.dram_tensor("p0_feat", [128, NTILE, 4, 128], BF16).ap()
    tokd = nc.dram_tensor("p0_tok", [T_ALL, 768], BF16).ap()
    decd = nc.dram_tensor("p0_dec", [128, NTILE, 4], F32).ap()
    ofw = nc.dram_tensor("p0_ofw", [T_ALL, 256], F32).ap()

    idt = nc.alloc_sbuf_tensor("idt", [128, 128], BF16)
    msk = nc.alloc_sbuf_tensor("msk", [128, 8, 128], BF16)
    gnt = nc.alloc_sbuf_tensor("gnt", [128, 256], F32)
    wbf = nc.alloc_sbuf_tensor("wbf", [128, 8, WCOLS], BF16)
    wa2f = nc.alloc_sbuf_tensor("wa2f", [64, 256], F32)
    wa2b = nc.alloc_sbuf_tensor("wa2b", [64, 256], BF16)
    epsb = nc.alloc_sbuf_tensor("epsb", [128, 1], F32)
    gmx = nc.alloc_sbuf_tensor("gmx", [128, 8], F32)
    scl = nc.alloc_sbuf_tensor("scl", [128, 2, 8], F32)
    bia = nc.alloc_sbuf_tensor("bia", [128, 2, 8], F32)
    p.dma("sp", idt[:], ident, writes=["idt"])
    p.dma("sp", msk[:], masks, writes=["msk"])
    p.dma("sp", gnt[:], gn, writes=["gnt"])
    p.dma("sp", wa2f[:], wa2, writes=["wa2f"])
    ba2t = nc.alloc_sbuf_tensor("ba2t", [128, 256], F32)
    p.dma("sp", ba2t[:], ba2, writes=["ba2t"])
    p.dma("sp", gmx[:], gmixT, writes=["gmx"])
    p.op("dve", lambda E: E.memset(epsb[:], 1e-6), writes=["epsb"])
    p.op("dve", lambda E: E.tensor_copy(out=wa2b[:], in_=wa2f[:]), reads=["wa2f"], writes=["wa2b"])

    with ExitStack() as es:
        mod = modvec(nc, p, es, cT, adaw, adabT, 16)
        for r in range(2):
            p.op("dve", lambda E, r=r: E.scalar_tensor_tensor(out=scl[:, r, :], in0=mod[:, 8:16, r], scalar=1.0, in1=gmx[:],
                                                              op0=ALU.add, op1=ALU.mult),
                 reads=["mod", "gmx"], writes=["scl"])
            p.op("dve", lambda E, r=r: E.tensor_copy(out=bia[:, r, :], in_=mod[:, 0:8, r]), reads=["mod"], writes=["bia"])
        wst = [es.enter_context(nc.sbuf_tensor("wst%d" % i, [128, WCOLS], F32)) for i in range(2)]
        wv = win.rearrange("(j q) n -> q j n", q=128)
        for j in range(8):
            s = j % 2
            p.dma("sp", wst[s][:], wv[:, j, :], writes=[("wst", s)])
            p.op("act", lambda E, s=s, j=j: E.activation(out=wbf[:, j, :], in_=wst[s][:], func=AF.Copy),
                 reads=[("wst", s)], writes=["wbf"])
        p.barrier()
    if stage == 'mod':
        return

    groups = [(2 * g, 2) for g in range(65)]
    if ngroups is not None:
        groups = groups[:ngroups]
    with ExitStack() as es:
        SB = lambda name, shape, dt: es.enter_context(nc.sbuf_tensor(name, shape, dt))
        PS = lambda name, dt=F32: es.enter_context(nc.psum_tensor(name, [128, 512] if dt == F32 else [128, 1024], dt))
        xt = [SB("xt%d" % i, [128, 2, D], F32) for i in range(2)]
        junk = SB("junk", [128, D], BF16)
        ssq = [SB("ssq%d" % i, [128, 12], F32) for i in range(2)]
        xn = [SB("xn%d" % i, [128, D], BF16) for i in range(2)]
        hT = [SB("hT%d" % i, [128, 8, 256], BF16) for i in range(2)]
        zaT = SB("zaT", [64, 256], BF16)
        oneb = SB("oneb", [128, 1], F32)
        ez = [SB("ez%d" % i, [128, 2, 128], F32) for i in range(2)]
        lap = [SB("lap%d" % i, [128, 2, 128], BF16) for i in range(2)]
        Ee = [SB("Ee%d" % i, [128, 6, 128], F32) for i in range(2)]
        fst = [SB("fst%d" % i, [128, 2, 4, 128], BF16) for i in range(2)]
        dst = [SB("dst%d" % i, [128, 2, 4], F32) for i in range(2)]
        tst = [SB("tst%d" % i, [128, 768], BF16) for i in range(2)]
        sg = [SB("sg%d" % i, [128, 256], F32) for i in range(2)]
        rsb = [SB("rsb%d" % i, [128, 256], F32) for i in range(2)]
        ptrA = PS("ptrA", BF16); ptrB = PS("ptrB", BF16)
        pqk = PS("pqk"); pza = PS("pza"); ptok0 = PS("ptok0"); ptok1 = PS("ptok1"); pla = PS("pla"); pcs = PS("pcs")
        p.op("dve", lambda E: E.memset(oneb[:], 1.0), writes=["oneb"])
        xv = xall.rearrange("(t q) d -> q t d", q=128)
        tt = 0
        for gi, (t0, nt) in enumerate(groups):
            s = gi % 2
            W = nt * 128
            r = 1 if gi == 0 else 0
            p.dma("sp", xt[s][:, 0:nt, :], xv[:, t0:t0 + nt, :], writes=[("xt", s)])
            for t in range(nt):
                p.op("act", lambda E, s=s, t=t: E.activation(out=junk[:], in_=xt[s][:, t, :], func=AF.Square, accum_out=ssq[s][:, t:t + 1]),
                     reads=[("xt", s)], writes=["junk", ("ssq", s)])
            p.op("act", lambda E, s=s, nt=nt: E.activation(out=ssq[s][:, 4:4 + nt], in_=ssq[s][:, 0:nt], func=AF.Ln, scale=1.0 / D, bias=epsb[:, 0:1]),
                 reads=[("ssq", s), "epsb"], writes=[("ssq", s)])
            p.op("act", lambda E, s=s, nt=nt: E.activation(out=ssq[s][:, 8:8 + nt], in_=ssq[s][:, 4:4 + nt], func=AF.Exp, scale=-0.5),
                 reads=[("ssq", s)], writes=[("ssq", s)])
            for t in range(nt):
                u = (tt + t) % 2
                p.op("dve", lambda E, s=s, t=t, u=u: E.tensor_scalar(out=xn[u][:], in0=xt[s][:, t, :], scalar1=ssq[s][:, 8 + t:9 + t], scalar2=None, op0=ALU.mult),
                     reads=[("xt", s), ("ssq", s)], writes=[("xn", u)])
                for j in range(8):
                    pt, key = (ptrA, "ptrA") if j < 4 else (ptrB, "ptrB")
                    p.op("pe", lambda E, u=u, j=j, pt=pt: E.transpose(out=pt[:, (j % 4) * 128:(j % 4 + 1) * 128], in_=xn[u][:, j * 128:(j + 1) * 128], identity=idt[:]),
                         reads=[("xn", u), "idt"], writes=[key])
                for j in range(8):
                    if j < 4:
                        p.op("dve", lambda E, s=s, t=t, j=j, r=r: E.tensor_scalar(out=hT[s][:, j, t * 128:(t + 1) * 128], in0=ptrA[:, j * 128:(j + 1) * 128],
                                                                                 scalar1=scl[:, r, j:j + 1], scalar2=bia[:, r, j:j + 1],
                                                                                 op0=ALU.mult, op1=ALU.add),
                             reads=["scl", "bia"], writes=[("hT", s), "ptrA"])
                    else:
                        p.op("act", lambda E, s=s, t=t, j=j, r=r: E.activation(out=hT[s][:, j, t * 128:(t + 1) * 128], in_=ptrB[:, (j - 4) * 128:(j - 3) * 128],
                                                                              func=AF.Identity, scale=scl[:, r, j:j + 1], bias=bia[:, r, j:j + 1]),
                             reads=["scl", "bia"], writes=[("hT", s), "ptrB"])
            if CUT < 2:
                tt += nt
                continue
            for (pt, key, o0, c0, M) in ((pqk, "pqk", 0, 0, 128), (pqk, "pqk", 256, 128, 128), (pza, "pza", 0, 768, 64)):
                for j in range(8):
                    p.op("pe", lambda E, pt=pt, o0=o0, c0=c0, M=M, j=j, s=s, W=W: E.matmul(pt[0:M, o0:o0 + W], lhsT=wbf[:, j, c0:c0 + M], rhs=hT[s][:, j, 0:W],
                                                                                          start=(j == 0), stop=(j == 7)),
                         reads=["wbf", ("hT", s)], writes=[key])
            p.op("act", lambda E, W=W: E.activation(out=zaT[:, 0:W], in_=pza[0:64, 0:W], func=AF.Copy), writes=["zaT", "pza"])
            for t in range(nt):
                u = (tt + t) % 2
                tg = t0 + t
                ts = slice(t * 128, (t + 1) * 128)
                if CUT < 3:
                    continue
                for (pt, key, c0, N) in ((ptok0, "ptok0", 128, 128), (ptok1, "ptok1", 256, 512)):
                    for j in range(8):
                        p.op("pe", lambda E, pt=pt, c0=c0, N=N, j=j, s=s, ts=ts: E.matmul(pt[:, 0:N], lhsT=hT[s][:, j, ts], rhs=wbf[:, j, c0:c0 + N],
                                                                                         start=(j == 0), stop=(j == 7)),
                             reads=["wbf", ("hT", s)], writes=[key])
                p.op("pe", lambda E, ts=ts: E.matmul(pla[:, 0:256], lhsT=zaT[:, ts], rhs=wa2b[:, :], start=True, stop=True),
                     reads=["zaT", "wa2b"], writes=["pla"])
                p.op("dve", lambda E, u=u: E.tensor_tensor(out=ez[u][:], in0=pla[:, 0:256], in1=ba2t[:], op=ALU.add), reads=["ba2t"], writes=[("ez", u), "pla"])
                p.op("act", lambda E, u=u: E.activation(out=ez[u][:], in_=ez[u][:], func=AF.Exp, scale=-1.0), writes=[("ez", u)])
                p.op("act", lambda E, u=u: E.activation(out=lap[u][:], in_=ez[u][:], func=AF.Ln, bias=oneb[:, 0:1]), reads=[("ez", u), "oneb"], writes=[("lap", u)])
                if CUT < 4:
                    continue
                for d in range(2):
                    p.op("pe", lambda E, d=d, u=u: E.matmul(pcs[:, d * 128:(d + 1) * 128], lhsT=lap[u][:, d, :], rhs=msk[:, d, :], start=True, stop=True),
                         reads=[("lap", u), "msk"], writes=["pcs"])
                    p.op("pe", lambda E, d=d, u=u: E.matmul(pcs[:, (2 + d) * 128:(3 + d) * 128], lhsT=msk[:, 2 + d, :], rhs=lap[u][:, d, :], start=True, stop=True),
                         reads=[("lap", u), "msk"], writes=["pcs"])
                p.op("act", lambda E, u=u: E.activation(out=Ee[u][:, 0:2, :], in_=pcs[:, 0:256], func=AF.Exp), writes=[("Ee", u), "pcs"])
                p.op("act", lambda E, u=u: E.activation(out=Ee[u][:, 2:4, :], in_=pcs[:, 0:256], func=AF.Exp, scale=-1.0), writes=[("Ee", u), "pcs"])
                p.op("act", lambda E, u=u: E.activation(out=Ee[u][:, 4:6, :], in_=pcs[:, 256:512], func=AF.Exp), writes=[("Ee", u), "pcs"])
                if CUT < 5:
                    continue
                p.op("act", lambda E, u=u: E.activation(out=sg[u][:], in_=ptok1[:, 256:512], func=AF.Exp, scale=-1.0), writes=[("sg", u), "ptok1"])
                p.op("act", lambda E, u=u: E.activation(out=rsb[u][:], in_=ptok1[:, 256:512], func=AF.Copy), writes=[("rsb", u), "ptok1"])
                p.op("act", lambda E, u=u: E.activation(out=tst[u][:, 256:512], in_=ptok1[:, 0:256], func=AF.Copy), writes=[("tst", u), "ptok1"])
                p.op("dve", lambda E, u=u: E.tensor_scalar_add(out=sg[u][:], in0=sg[u][:], scalar1=1.0), writes=[("sg", u)])
                p.op("dve", lambda E, u=u: E.reciprocal(out=sg[u][:], in_=sg[u][:]), writes=[("sg", u)])
                p.op("dve", lambda E, u=u: E.tensor_tensor(out=tst[u][:, 512:768], in0=rsb[u][:], in1=sg[u][:], op=ALU.mult),
                     reads=[("sg", u), ("rsb", u)], writes=[("tst", u)])
                for d in range(2):
                    p.op("dve", lambda E, d=d, u=u, s=s, t=t, ts=ts: E.scalar_tensor_tensor(out=fst[s][:, t, 2 * d, :], in0=pqk[:, ts], scalar=float(128 ** -0.5), in1=Ee[u][:, d, :],
                                                                                          op0=ALU.mult, op1=ALU.mult),
                         reads=[("Ee", u)], writes=[("fst", s), "pqk"])
                    p.op("dve", lambda E, d=d, u=u, s=s, t=t: E.tensor_tensor(out=fst[s][:, t, 2 * d + 1, :], in0=pqk[:, 256 + t * 128:256 + (t + 1) * 128], in1=Ee[u][:, 2 + d, :], op=ALU.mult),
                         reads=[("Ee", u)], writes=[("fst", s), "pqk"])
                    p.op("dve", lambda E, d=d, u=u: E.tensor_tensor(out=tst[u][:, 128 * d:128 * (d + 1)], in0=ptok0[:, 0:128], in1=Ee[u][:, 4 + d, :], op=ALU.mult),
                         reads=[("Ee", u)], writes=[("tst", u), "ptok0"])
                p.op("dve", lambda E, u=u, s=s, t=t: E.tensor_copy(out=dst[s][:, t, 0:2], in_=Ee[u][:, 0, 63:128:64]), reads=[("Ee", u)], writes=[("dst", s)])
                p.op("dve", lambda E, u=u, s=s, t=t: E.tensor_copy(out=dst[s][:, t, 2:4], in_=Ee[u][:, 1, 0:65:64]), reads=[("Ee", u)], writes=[("dst", s)])
                if CUT < 6:
                    continue
                p.dma("pool", tokd[tg * 128:(tg + 1) * 128, :], tst[u][:], reads=[("tst", u)], writes=[("tokd", tg)])
            if CUT < 6:
                tt += nt
                continue
            p.dma("pool", feat[:, t0:t0 + nt, :, :], fst[s][:, 0:nt, :, :], reads=[("fst", s)], writes=[("feat", gi)])
            p.dma("pool", decd[:, t0:t0 + nt, :], dst[s][:, 0:nt, :], reads=[("dst", s)], writes=[("decd", gi)])
            tt += nt
        p.barrier()
    if stage == 'a':
        return

    ntl = sum(nt for _, nt in groups)
    tile2group = {}
    for gi, (t0, nt) in enumerate(groups):
        for t in range(nt):
            tile2group[t0 + t] = gi
    lat_tiles = [t for t in range(2, ntl)]
    order_f = [0, 1] + lat_tiles
    order_r = [1, 0] + lat_tiles[::-1]
    with ExitStack() as es:
        SBs = lambda name, shape, dt: es.enter_context(nc.sbuf_tensor(name, shape, dt))
        fl = [SBs("fl%d" % i, [128, 4, 128], BF16) for i in range(3)]
        tl = [SBs("tl%d" % i, [128, 768], BF16) for i in range(3)]
        dl = [SBs("dl%d" % i, [128, 4], F32) for i in range(3)]
        ol = [SBs("ol%d" % i, [128, 256], F32) for i in range(3)]
        sTm = [SBs("sTm%d" % i, [128, 128], BF16) for i in range(2)]
        S32 = SBs("S32", [128, 256], F32)
        Sbf = [SBs("Sbf%d" % i, [128, 256], BF16) for i in range(3)]
        ost = [SBs("ost%d" % i, [128, 256], F32) for i in range(2)]
        osq = SBs("osq", [128, 256], BF16)
        rs = [SBs("rs%d" % i, [128, 4], F32) for i in range(2)]
        yb = [SBs("yb%d" % i, [128, 256], BF16) for i in range(2)]
        yts = [SBs("yts%d" % i, [128, 2, 128], BF16) for i in range(2)]
        psc = [es.enter_context(nc.psum_tensor("psc%d" % i, [128, 512], F32)) for i in range(2)]
        po = [es.enter_context(nc.psum_tensor("po%d" % i, [128, 512], F32)) for i in range(2)]
        pS = [es.enter_context(nc.psum_tensor("pS%d" % i, [128, 512], F32)) for i in range(2)]
        pyT = es.enter_context(nc.psum_tensor("pyT", [128, 8, 128], BF16))
        yTv = yT.rearrange("(c q) t -> q c t", q=128)
        for dirn, order in ((0, order_f), (1, order_r))[:(1 if stage == 'b' else 2)]:
            p.op("dve", lambda E: E.memset(S32[:], 0.0), writes=["S32"])
            sb = 0
            p.op("dve", lambda E, sb=sb: E.memset(Sbf[sb][:], 0.0), writes=[("Sbf", sb)])
            chunk_order = (0, 1) if dirn == 0 else (1, 0)
            for n, tg in enumerate(order):
                if tg == 2 or (dirn == 1 and n == 2):
                    pass
                s3 = n % 3
                s2 = n % 2
                gi = tile2group[tg]
                p.dma("sp", fl[s3][:], feat[:, tg, :, :], reads=[("feat", gi)], writes=[("fl", s3)])
                p.dma("sp", tl[s3][:], tokd[tg * 128:(tg + 1) * 128, :], reads=[("tokd", tg)], writes=[("tl", s3)])
                p.dma("sp", dl[s3][:], decd[:, tg, :], reads=[("decd", gi)], writes=[("dl", s3)])
                if dirn == 1:
                    p.dma("sp", ol[s3][:], ofw[tg * 128:(tg + 1) * 128, :], reads=[("ofw", tg)], writes=[("ol", s3)])
                qin = fl[s3][:, 2 * dirn, :]
                kneg = fl[s3][:, 2 * dirn + 1, :]
                kst = tl[s3][:, 128 * dirn:128 * (dirn + 1)]
                vv = tl[s3][:, 256:512]
                p.op("pe", lambda E, s2=s2, kneg=kneg, qin=qin: E.matmul(psc[s2][:, 0:128], lhsT=kneg, rhs=qin, start=True, stop=True),
                     reads=[("fl", s3)], writes=[("psc", s2)])
                p.op("dve", lambda E, s2=s2, dirn=dirn: E.tensor_tensor(out=sTm[s2][:], in0=psc[s2][:, 0:128], in1=msk[:, 4 + dirn, :], op=ALU.mult),
                     reads=["msk"], writes=[("sTm", s2), ("psc", s2)])
                p.op("pe", lambda E, s2=s2, vv=vv: E.matmul(po[s2][:, 0:256], lhsT=sTm[s2][:], rhs=vv, start=True, stop=False),
                     reads=[("sTm", s2), ("tl", s3)], writes=[("po", s2)])
                for ci, c in enumerate(chunk_order):
                    cs = slice(64 * c, 64 * (c + 1))
                    p.op("pe", lambda E, s2=s2, s3=s3, cs=cs, sb=sb, dirn=dirn, ci=ci: E.matmul(po[s2][cs, 0:256], lhsT=fl[s3][:, 2 * dirn, cs], rhs=Sbf[sb][:],
                                                                                             start=False, stop=True),
                         reads=[("fl", s3), ("Sbf", sb)], writes=[("po", s2)])
                    pi = (2 * n + ci) % 2
                    p.op("pe", lambda E, pi=pi, s3=s3, cs=cs, dirn=dirn: E.matmul(pS[pi][:, 0:256], lhsT=tl[s3][cs, 128 * dirn:128 * (dirn + 1)], rhs=tl[s3][cs, 256:512],
                                                                               start=True, stop=True),
                         reads=[("tl", s3)], writes=[("pS", pi)])
                    p.op("dve", lambda E, pi=pi, s3=s3, c=c, dirn=dirn: E.scalar_tensor_tensor(out=S32[:], in0=S32[:], scalar=dl[s3][:, 2 * dirn + c:2 * dirn + c + 1], in1=pS[pi][:, 0:256],
                                                                                            op0=ALU.mult, op1=ALU.add),
                         reads=[("dl", s3)], writes=["S32", ("pS", pi)])
                    sb = (sb + 1) % 3
                    p.op("act", lambda E, sb=sb: E.activation(out=Sbf[sb][:], in_=S32[:], func=AF.Copy), reads=["S32"], writes=[("Sbf", sb)])
                if dirn == 0:
                    p.op("act", lambda E, s2=s2: E.activation(out=ost[s2][:], in_=po[s2][:, 0:256], func=AF.Copy), writes=[("ost", s2), ("po", s2)])
                    p.dma("pool", ofw[tg * 128:(tg + 1) * 128, :], ost[s2][:], reads=[("ost", s2)], writes=[("ofw", tg)])
                else:
                    p.op("dve", lambda E, s2=s2, s3=s3: E.tensor_tensor(out=ost[s2][:], in0=po[s2][:, 0:256], in1=ol[s3][:], op=ALU.add),
                         reads=[("ol", s3)], writes=[("ost", s2), ("po", s2)])
                    p.op("act", lambda E, s2=s2: E.activation(out=osq[:], in_=ost[s2][:], func=AF.Square, accum_out=rs[s2][:, 0:1]),
                         reads=[("ost", s2)], writes=["osq", ("rs", s2)])
                    p.op("act", lambda E, s2=s2: E.activation(out=rs[s2][:, 1:2], in_=rs[s2][:, 0:1], func=AF.Ln, scale=1.0 / 256, bias=epsb[:, 0:1]),
                         reads=[("rs", s2), "epsb"], writes=[("rs", s2)])
                    p.op("act", lambda E, s2=s2: E.activation(out=rs[s2][:, 2:3], in_=rs[s2][:, 1:2], func=AF.Exp, scale=-0.5),
                         reads=[("rs", s2)], writes=[("rs", s2)])
                    p.op("dve", lambda E, s2=s2: E.scalar_tensor_tensor(out=ost[s2][:], in0=ost[s2][:], scalar=rs[s2][:, 2:3], in1=gnt[:], op0=ALU.mult, op1=ALU.mult),
                         reads=[("ost", s2), ("rs", s2), "gnt"], writes=[("ost", s2)])
                    p.op("dve", lambda E, s2=s2, s3=s3: E.tensor_tensor(out=yb[s2][:], in0=ost[s2][:], in1=tl[s3][:, 512:768], op=ALU.mult),
                         reads=[("ost", s2), ("tl", s3)], writes=[("yb", s2)])
                    for c in range(2):
                        p.op("pe", lambda E, s2=s2, c=c: E.transpose(out=pyT[:, c, :], in_=yb[s2][:, c * 128:(c + 1) * 128], identity=idt[:]),
                             reads=[("yb", s2), "idt"], writes=["pyT"])
                    p.op("act", lambda E, s2=s2: E.activation(out=yts[s2][:], in_=pyT[:, 0:2, :], func=AF.Copy), writes=[("yts", s2), "pyT"])
                    p.dma("pool", yTv[:, :, tg * 128:(tg + 1) * 128], yts[s2][:], reads=[("yts", s2)], writes=[("yT", tg)])
        p.barrier()


D = 1024
DFF = 2816
NFC = 22
NLOC = 66 * 64
NOWN = 4096
GELU_C = 0.7978845608028654


def build_p13(nc, p, io, has_ctx, final, nblocks=4):
    NM = 48 if has_ctx else 32
    x1d = nc.dram_tensor("x1d", [NLOC + 256, D], F32).ap()
    idt = nc.alloc_sbuf_tensor("idt", [128, 128], BF16)
    idf = nc.alloc_sbuf_tensor("idf", [128, 128], F32)
    epsb = nc.alloc_sbuf_tensor("epsb", [128, 1], F32)
    gff = nc.alloc_sbuf_tensor("gff", [128, 8], F32)
    hvt = nc.alloc_sbuf_tensor("hvt", [128, 2], F32)
    cw = nc.alloc_sbuf_tensor("cw", [128, NFC, 9], F32)
    cb = nc.alloc_sbuf_tensor("cb", [128, NFC], F32)
    scl2 = nc.alloc_sbuf_tensor("scl2", [128, 2, 8], F32)
    bia2 = nc.alloc_sbuf_tensor("bia2", [128, 2, 8], F32)
    scl3 = nc.alloc_sbuf_tensor("scl3", [128, 2, 8], F32)
    bia3 = nc.alloc_sbuf_tensor("bia3", [128, 2, 8], F32)
    gbc = nc.alloc_sbuf_tensor("gbc", [128, 4, D], F32)
    h2Td = nc.dram_tensor("h2Td", [128, 8, NLOC + 256], BF16).ap()
    p.dma("sp", idt[:], io["ident"], writes=["idt"])
    p.dma("sp", idf[:], io["identf"], writes=["idf"])
    p.dma("sp", gff[:], io["gffnT"], writes=["gff"])
    p.dma("sp", hvt[:], io["hv"], writes=["hvt"])
    p.dma("sp", cw[:], io["convw"], writes=["cw"])
    p.dma("sp", cb[:], io["convb"], writes=["cb"])
    p.op("dve", lambda E: E.memset(epsb[:], 1e-6), writes=["epsb"])
    if final:
        fgb = nc.alloc_sbuf_tensor("fgb", [128, D], F32)
        p.dma("sp", fgb[:], io["fgbc"], writes=["fgb"])
    else:
        gnx = nc.alloc_sbuf_tensor("gnx", [128, 8], F32)
        p.dma("sp", gnx[:], io["gnextT"], writes=["gnx"])

    with ExitStack() as es:
        SB = lambda name, shape, dt: es.enter_context(nc.sbuf_tensor(name + "_m", shape, dt))
        mod = modvec(nc, p, es, io["cT"], io["adaw"], io["adabT"], NM)
        for r in range(2):
            p.op("dve", lambda E, r=r: E.scalar_tensor_tensor(out=scl2[:, r, :], in0=mod[:, 16:24, r], scalar=1.0, in1=gff[:], op0=ALU.add, op1=ALU.mult),
                 reads=["mod", "gff"], writes=["scl2"])
            p.op("dve", lambda E, r=r: E.tensor_copy(out=bia2[:, r, :], in_=mod[:, 8:16, r]), reads=["mod"], writes=["bia2"])
            if has_ctx:
                p.op("dve", lambda E, r=r: E.scalar_tensor_tensor(out=scl3[:, r, :], in0=mod[:, 40:48, r], scalar=1.0, in1=gnx[:], op0=ALU.add, op1=ALU.mult),
                     reads=["mod", "gnx"], writes=["scl3"])
                p.op("dve", lambda E, r=r: E.tensor_copy(out=bia3[:, r, :], in_=mod[:, 32:40, r]), reads=["mod"], writes=["bia3"])
        ones = SB("ones", [128, 128], F32)
        tmpb = [SB("tmpb%d" % i, [128, 128], F32) for i in range(2)]
        pb = [es.enter_context(nc.psum_tensor("pb%d" % i, [128, 512], F32)) for i in range(2)]
        p.op("dve", lambda E: E.memset(ones[:], 1.0), writes=["ones"])
        k = 0
        for v, (c0, r) in enumerate(((0, 0), (24, 0), (0, 1), (24, 1))):
            if r == 1 and not has_ctx:
                continue
            for j in range(8):
                u = k % 2
                k += 1
                p.op("dve", lambda E, u=u, c0=c0, j=j, r=r: E.tensor_scalar(out=tmpb[u][:], in0=ones[:], scalar1=mod[:, c0 + j, r:r + 1], scalar2=None, op0=ALU.mult),
                     reads=["ones", "mod"], writes=[("tmpb", u)])
                p.op("pe", lambda E, u=u: E.matmul(pb[u][:, 0:128], lhsT=tmpb[u][:], rhs=idf[:], start=True, stop=True),
                     reads=[("tmpb", u), "idf"], writes=[("pb", u)])
                p.op("act", lambda E, u=u, v=v, j=j: E.activation(out=gbc[:, v, j * 128:(j + 1) * 128], in_=pb[u][:, 0:128], func=AF.Copy),
                     writes=["gbc", ("pb", u)])
        p.barrier()

    def norm_mod_T(es_tiles, xsrc, xkey, sclt, biat, r, dst_fn, dkey, ptr, pkey):
        junk, ssq, xn = es_tiles
        p.op("act", lambda E: E.activation(out=junk[:], in_=xsrc, func=AF.Square, accum_out=ssq[:, 0:1]), reads=[xkey], writes=["junk", "ssq"])
        p.op("act", lambda E: E.activation(out=ssq[:, 1:2], in_=ssq[:, 0:1], func=AF.Ln, scale=1.0 / D, bias=epsb[:, 0:1]), reads=["epsb"], writes=["ssq"])
        p.op("act", lambda E: E.activation(out=ssq[:, 2:3], in_=ssq[:, 1:2], func=AF.Exp, scale=-0.5), writes=["ssq"])
        p.op("dve", lambda E: E.tensor_scalar(out=xn[:], in0=xsrc, scalar1=ssq[:, 2:3], scalar2=None, op0=ALU.mult), reads=[xkey, "ssq"], writes=["xn"])
        for j in range(8):
            p.op("pe", lambda E, j=j: E.transpose(out=ptr[:, j, :], in_=xn[:, j * 128:(j + 1) * 128], identity=idt[:]), reads=["xn", "idt"], writes=[pkey])
        for j in range(8):
            p.op("dve", lambda E, j=j: E.tensor_scalar(out=dst_fn(j), in0=ptr[:, j, :], scalar1=sclt[:, r, j:j + 1], scalar2=biat[:, r, j:j + 1], op0=ALU.mult, op1=ALU.add),
                 reads=["scl2", "bia2", "scl3", "bia3"], writes=[dkey, pkey])

    ntA = 33 + (2 if has_ctx else 0)
    with ExitStack() as es:
        SB = lambda name, shape, dt: es.enter_context(nc.sbuf_tensor(name + "_a", shape, dt))
        yt = [SB("yt%d" % i, [128, 8, 128], BF16) for i in range(2)]
        xt = [SB("xt%d" % i, [128, D], F32) for i in range(2)]
        x1 = [SB("x1_%d" % i, [128, D], F32) for i in range(2)]
        junk = SB("junk", [128, D], BF16); ssq = SB("ssq", [128, 4], F32); xn = SB("xn", [128, D], BF16)
        po = [es.enter_context(nc.psum_tensor("po_a%d" % i, [128, 512], F32)) for i in range(2)]
        ptr = es.enter_context(nc.psum_tensor("ptr_a", [128, 8, 128], BF16))
        woutb = SB("woutb", [128, 8, D], BF16)
        wst = [SB("wst%d" % i, [128, D], F32) for i in range(2)]
        hsa = [SB("hsa%d" % i, [128, 8, 128], BF16) for i in range(2)]
        wv = io["wout"].rearrange("(j q) n -> q j n", q=128)
        for j in range(8):
            s_ = j % 2
            p.dma("sp", wst[s_][:], wv[:, j, :], writes=[("wst", s_)])
            p.op("act", lambda E, s_=s_, j=j: E.activation(out=woutb[:, j, :], in_=wst[s_][:], func=AF.Copy), reads=[("wst", s_)], writes=["woutb"])
        yv = io["yT4"].rearrange("h (c q) t -> q (h c) t", q=128)
        for t in range(ntA):
            s = t % 2
            r = 1 if t >= 33 else 0
            p.dma("sp", yt[s][:], yv[:, :, t * 128:(t + 1) * 128], writes=[("yt", s)])
            p.dma("sp", xt[s][:], io["xin"][t * 128:(t + 1) * 128, :], writes=[("xt", s)])
            for hf in range(2):
                for kc in range(8):
                    p.op("pe", lambda E, hf=hf, kc=kc, s=s: E.matmul(po[hf][:, :], lhsT=yt[s][:, kc, :], rhs=woutb[:, kc, hf * 512:(hf + 1) * 512], start=(kc == 0), stop=(kc == 7)),
                         reads=[("yt", s), "woutb"], writes=[("po", hf)])
            for hf in range(2):
                hs = slice(hf * 512, (hf + 1) * 512)
                p.op("dve", lambda E, hf=hf, hs=hs, s=s, r=r: E.tensor_tensor(out=x1[s][:, hs], in0=po[hf][:, :], in1=gbc[:, 2 * r, hs], op=ALU.mult),
                     reads=["gbc"], writes=[("x1", s), ("po", hf)])
                p.op("dve", lambda E, hs=hs, s=s: E.tensor_tensor(out=x1[s][:, hs], in0=x1[s][:, hs], in1=xt[s][:, hs], op=ALU.add),
                     reads=[("xt", s)], writes=[("x1", s)])
            p.dma("pool", x1d[t * 128:(t + 1) * 128, :], x1[s][:], reads=[("x1", s)], writes=[("x1d", t)])
            norm_mod_T((junk, ssq, xn), x1[s][:], ("x1", s), scl2, bia2, r, lambda j, s=s: hsa[s][:, j, :], ("hsa", s), ptr, "ptr")
            p.dma("pool", h2Td[:, :, t * 128:(t + 1) * 128], hsa[s][:], reads=[("hsa", s)])
        p.barrier()

    with ExitStack() as es:
        SB = lambda name, shape, dt: es.enter_context(nc.sbuf_tensor(name + "_b", shape, dt))
        PSB = lambda name: es.enter_context(nc.psum_tensor(name + "_b", [128, 512], F32))
        wdres = SB("wdres", [128, NFC, D], BF16)
        wgs = SB("wgs", [128, 8, 256], F32)
        wgb = [SB("wgb%d" % i, [128, 8, 256], BF16) for i in range(2)]
        h2b = SB("h2b", [128, 8, 1152], BF16)
        gs = SB("gs", [128, 18, 64], F32)
        acc = SB("acc", [128, 16, 64], F32)
        t1 = SB("t1", [128, 16, 64], F32)
        t2 = SB("t2", [128, 16, 64], F32)
        uT = SB("uT", [128, NFC, 1024], BF16)
        xr = SB("xr", [128, D], F32)
        x2 = [SB("x2_%d" % i, [128, D], F32) for i in range(2)]
        junk = SB("junk", [128, D], BF16); ssq = SB("ssq", [128, 4], F32); xn = SB("xn", [128, D], BF16)
        hst = [SB("hst%d" % i, [128, 8, 128], BF16) for i in range(2)]
        pg = [PSB("pg%d" % i) for i in range(3)]
        pa = [PSB("pa%d" % i) for i in range(2)]
        py = [PSB("py%d" % i) for i in range(2)]
        ptr = es.enter_context(nc.psum_tensor("ptr_b", [128, 8, 128], BF16))
        wupv = io["wup"].rearrange("(j q) n -> q j n", q=128)
        for fc in range(NFC):
            s = fc % 2
            p.dma("sp", x2[s][:], io["wdn"][fc * 128:(fc + 1) * 128, :], writes=[("x2", s)])
            p.op("act", lambda E, s=s, fc=fc: E.activation(out=wdres[:, fc, :], in_=x2[s][:], func=AF.Copy), reads=[("x2", s)], writes=["wdres"])
        wcnt = 0
        xcnt = 0
        blocks = [("lat", bi) for bi in range(nblocks)] + ([("ctx", 0)] if has_ctx else [])
        for kind, bi in blocks:
            if kind == "lat":
                gtok0 = 16 * bi * 64
                nld = 1152
                ggroups = [(384 * i, 384) for i in range(3)]
                agroups = [(64 + 512 * i, 512) for i in range(2)]
                ntok = 1024
                r = 0
            else:
                gtok0 = NLOC
                nld = 256
                ggroups = [(0, 256)]
                agroups = [(0, 256)]
                ntok = 256
                r = 1
            p.dma("sp", h2b[:, :, 0:nld], h2Td[:, :, gtok0:gtok0 + nld], writes=["h2b"])
            for fc in range(NFC):
                ws = wcnt % 2
                wcnt += 1
                p.dma("sp", wgs[:, :, 0:128], wupv[:, :, DFF + fc * 128:DFF + (fc + 1) * 128], writes=["wgs"])
                p.dma("sp", wgs[:, :, 128:256], wupv[:, :, fc * 128:(fc + 1) * 128], writes=["wgs"])
                p.op("act", lambda E, ws=ws: E.activation(out=wgb[ws][:], in_=wgs[:], func=AF.Copy), reads=["wgs"], writes=[("wgb", ws)])
                for gi_, (tk, n) in enumerate(ggroups):
                    for j in range(8):
                        p.op("pe", lambda E, gi_=gi_, tk=tk, n=n, j=j, ws=ws: E.matmul(pg[gi_][:, 0:n], lhsT=wgb[ws][:, j, 0:128], rhs=h2b[:, j, tk:tk + n], start=(j == 0), stop=(j == 7)),
                             reads=[("wgb", ws), "h2b"], writes=[("pg", gi_)])
                for ai, (tk, n) in enumerate(agroups):
                    for j in range(8):
                        p.op("pe", lambda E, ai=ai, tk=tk, n=n, j=j, ws=ws: E.matmul(pa[ai][:, 0:n], lhsT=wgb[ws][:, j, 128:256], rhs=h2b[:, j, tk:tk + n], start=(j == 0), stop=(j == 7)),
                             reads=[("wgb", ws), "h2b"], writes=[("pa", ai)])
                if kind == "lat":
                    for gi_ in range(3):
                        p.op("act", lambda E, gi_=gi_: E.activation(out=gs[:, 6 * gi_:6 * gi_ + 6, :], in_=pg[gi_][:, 0:384], func=AF.Copy), writes=["gs", ("pg", gi_)])
                    if bi == 0:
                        p.op("act", lambda E: E.activation(out=gs[:, 0, :], in_=gs[:, 0, :], func=AF.Copy, scale=hvt[:, 0:1]), reads=["hvt"], writes=["gs"])
                    if bi == 3:
                        p.op("act", lambda E: E.activation(out=gs[:, 17, :], in_=gs[:, 17, :], func=AF.Copy, scale=hvt[:, 1:2]), reads=["hvt"], writes=["gs"])
                    p.op("dve", lambda E, fc=fc: E.tensor_scalar(out=acc[:], in0=gs[:, 1:17, :], scalar1=cw[:, fc, 4:5], scalar2=cb[:, fc:fc + 1], op0=ALU.mult, op1=ALU.add),
                         reads=["gs", "cw", "cb"], writes=["acc"])
                    for dy in (-1, 0, 1):
                        for dx in (-1, 0, 1):
                            if dy == 0 and dx == 0:
                                continue
                            kidx = (dy + 1) * 3 + (dx + 1)
                            oc0, oc1 = (1, 64) if dx == -1 else ((0, 63) if dx == 1 else (0, 64))
                            ic0, ic1 = oc0 + dx, oc1 + dx
                            p.op("dve", lambda E, fc=fc, kidx=kidx, dy=dy, oc0=oc0, oc1=oc1, ic0=ic0, ic1=ic1:
                                 E.scalar_tensor_tensor(out=acc[:, :, oc0:oc1], in0=gs[:, 1 + dy:17 + dy, ic0:ic1], scalar=cw[:, fc, kidx:kidx + 1], in1=acc[:, :, oc0:oc1],
                                                        op0=ALU.mult, op1=ALU.add),
                                 reads=["gs", "cw"], writes=["acc"])
                    av, t1v, t2v = acc[:], t1[:], t2[:]
                    uviews = [(uT[:, fc, 512 * i:512 * (i + 1)], t2[:, 8 * i:8 * i + 8, :].rearrange("q a b -> q (a b)"), pa[i][:, 0:512], ("pa", i)) for i in range(2)]
                else:
                    gsc = gs[:, 0:4, :].rearrange("q a b -> q (a b)")
                    accc = acc[:, 0:4, :].rearrange("q a b -> q (a b)")
                    p.op("act", lambda E, gsc=gsc: E.activation(out=gsc, in_=pg[0][:, 0:256], func=AF.Copy), writes=["gs", ("pg", 0)])
                    p.op("dve", lambda E, fc=fc, gsc=gsc, accc=accc: E.tensor_scalar(out=accc, in0=gsc, scalar1=cw[:, fc, 4:5], scalar2=cb[:, fc:fc + 1], op0=ALU.mult, op1=ALU.add),
                         reads=["gs", "cw", "cb"], writes=["acc"])
                    p.op("dve", lambda E, fc=fc, gsc=gsc, accc=accc: E.scalar_tensor_tensor(out=accc[:, 1:256], in0=gsc[:, 0:255], scalar=cw[:, fc, 3:4], in1=accc[:, 1:256], op0=ALU.mult, op1=ALU.add),
                         reads=["gs", "cw"], writes=["acc"])
                    p.op("dve", lambda E, fc=fc, gsc=gsc, accc=accc: E.scalar_tensor_tensor(out=accc[:, 0:255], in0=gsc[:, 1:256], scalar=cw[:, fc, 5:6], in1=accc[:, 0:255], op0=ALU.mult, op1=ALU.add),
                         reads=["gs", "cw"], writes=["acc"])
                    av, t1v, t2v = acc[:, 0:4, :], t1[:, 0:4, :], t2[:, 0:4, :]
                    uviews = [(uT[:, fc, 0:256], t2[:, 0:4, :].rearrange("q a b -> q (a b)"), pa[0][:, 0:256], ("pa", 0))]
                p.op("act", lambda E, av=av, t1v=t1v: E.activation(out=t1v, in_=av, func=AF.Square), reads=["acc"], writes=["t1"])
                p.op("dve", lambda E, t1v=t1v: E.tensor_scalar(out=t1v, in0=t1v, scalar1=0.044715, scalar2=1.0, op0=ALU.mult, op1=ALU.add), writes=["t1"])
                p.op("dve", lambda E, t1v=t1v, av=av: E.tensor_tensor(out=t1v, in0=t1v, in1=av, op=ALU.mult), reads=["acc"], writes=["t1"])
                p.op("act", lambda E, t1v=t1v: E.activation(out=t1v, in_=t1v, func=AF.Tanh, scale=GELU_C), writes=["t1"])
                p.op("dve", lambda E, t1v=t1v, av=av, t2v=t2v: E.scalar_tensor_tensor(out=t2v, in0=t1v, scalar=1.0, in1=av, op0=ALU.add, op1=ALU.mult),
                     reads=["t1", "acc"], writes=["t2"])
                for (uo, tin2, pav, pkey) in uviews:
                    p.op("dve", lambda E, uo=uo, tin2=tin2, pav=pav: E.scalar_tensor_tensor(out=uo, in0=tin2, scalar=0.5, in1=pav, op0=ALU.mult, op1=ALU.mult),
                         reads=["t2"], writes=["uT", pkey])
            for tl_ in range(ntok // 128):
                for hf in range(2):
                    for fc in range(NFC):
                        p.op("pe", lambda E, tl_=tl_, fc=fc, hf=hf: E.matmul(py[hf][:, :], lhsT=uT[:, fc, tl_ * 128:(tl_ + 1) * 128], rhs=wdres[:, fc, hf * 512:(hf + 1) * 512],
                                                                           start=(fc == 0), stop=(fc == NFC - 1)),
                             reads=["uT", "wdres"], writes=[("py", hf)])
                s = xcnt % 2
                xcnt += 1
                if kind == "lat":
                    ltok = (16 * bi + 1) * 64 + tl_ * 128
                    otok = ltok - 64
                else:
                    ltok = NLOC + tl_ * 128
                    otok = NOWN + tl_ * 128
                p.dma("sp", xr[:], x1d[ltok:ltok + 128, :], writes=["xr"])
                for hf in range(2):
                    hs = slice(hf * 512, (hf + 1) * 512)
                    p.op("dve", lambda E, hf=hf, hs=hs, s=s, r=r: E.tensor_tensor(out=x2[s][:, hs], in0=py[hf][:, :], in1=gbc[:, 2 * r + 1, hs], op=ALU.mult),
                         reads=["gbc"], writes=[("x2", s), ("py", hf)])
                    p.op("dve", lambda E, hs=hs, s=s: E.tensor_tensor(out=x2[s][:, hs], in0=x2[s][:, hs], in1=xr[:, hs], op=ALU.add),
                         reads=["xr"], writes=[("x2", s)])
                if final:
                    p.op("act", lambda E, s=s: E.activation(out=junk[:], in_=x2[s][:], func=AF.Square, accum_out=ssq[:, 0:1]), reads=[("x2", s)], writes=["junk", "ssq"])
                    p.op("act", lambda E: E.activation(out=ssq[:, 1:2], in_=ssq[:, 0:1], func=AF.Ln, scale=1.0 / D, bias=epsb[:, 0:1]), reads=["epsb"], writes=["ssq"])
                    p.op("act", lambda E: E.activation(out=ssq[:, 2:3], in_=ssq[:, 1:2], func=AF.Exp, scale=-0.5), writes=["ssq"])
                    p.op("dve", lambda E, s=s: E.scalar_tensor_tensor(out=x2[s][:], in0=x2[s][:], scalar=ssq[:, 2:3], in1=fgb[:], op0=ALU.mult, op1=ALU.mult),
                         reads=["ssq", "fgb"], writes=[("x2", s)])
                    p.dma("pool", io["out"][otok:otok + 128, :], x2[s][:], reads=[("x2", s)])
                else:
                    if kind == "lat":
                        p.dma("pool", io["x2"][otok:otok + 128, :], x2[s][:], reads=[("x2", s)])
                    norm_mod_T((junk, ssq, xn), x2[s][:], ("x2", s), scl3, bia3, r, lambda j, s=s: hst[s][:, j, :], ("hst", s), ptr, "ptr")
                    p.dma("pool", io["hnT"][:, :, otok:otok + 128], hst[s][:], reads=[("hst", s)])
        p.barrier()


T2 = 16640
NT2 = 130
D = 1024
WC2 = 800
TOKW = 641
SEG = 1280
NSEG = 13


def p2_consts():
    I = np.eye(128, dtype=np.float32)
    s = np.arange(128)[:, None]
    t = np.arange(128)[None, :]
    same = (s // 64) == (t // 64)
    m = np.zeros((128, 2, 128), np.float32)
    m[:, 0] = np.where(same & (t >= s), 1.0, 0)
    m[:, 1] = np.where(same & (t <= s), 1.0, 0)
    return I.astype(ml_dtypes.bfloat16), I, I[::-1].copy(), m


def p2_inputs(inp, core, hT_all):
    b, h = core // 4, core % 4
    f = np.float32
    w = inp["mlstm_w_in"][0]
    win = np.zeros((1024, WC2), f)
    win[:, 0:128] = w[:, h * 128:(h + 1) * 128]
    win[:, 128:256] = w[:, 512 + h * 128:512 + (h + 1) * 128]
    win[:, 256:512] = w[:, 1024 + h * 256:1024 + (h + 1) * 256]
    win[:, 512:768] = w[:, 2048 + h * 256:2048 + (h + 1) * 256]
    for g in range(4):
        win[:, 768 + g] = w[:, 3072 + g * 4 + h]
    bg = np.array([inp["mlstm_b_gate"][0][g * 4 + h] for g in range(4)], f)
    cwv = inp["mlstm_conv_w"][0]
    cbv = inp["mlstm_conv_b"][0]
    cwq = np.zeros((128, 8), f)
    cwq[:, 0:3] = cwv[:, h * 128:(h + 1) * 128].T
    cwq[:, 3] = cbv[h * 128:(h + 1) * 128]
    cwq[:, 4:7] = cwv[:, 512 + h * 128:512 + (h + 1) * 128].T
    cwq[:, 7] = cbv[512 + h * 128:512 + (h + 1) * 128]
    identb, identf, jf, masks = p2_consts()
    return {"hT": hT_all, "win": win, "bg": np.ascontiguousarray(np.broadcast_to(bg[None, :], (128, 4))), "cwq": cwq,
            "gn": np.ascontiguousarray(np.broadcast_to(inp["mlstm_norm_g"][0][h * 256:(h + 1) * 256][None, :], (128, 256))),
            "ident": identb, "identf": identf, "jf": jf, "masks": masks.astype(ml_dtypes.bfloat16)}


def p2_decl(nc):
    dt = lambda n, s, t: nc.dram_tensor(n, s, t, kind="ExternalInput").ap()
    io = {"hT": dt("hT", [128, 8, T2], BF16), "win": dt("win", [1024, WC2], F32), "bg": dt("bg", [128, 4], F32), "cwq": dt("cwq", [128, 8], F32),
          "gn": dt("gn", [128, 256], F32), "ident": dt("ident", [128, 128], BF16), "identf": dt("identf", [128, 128], F32),
          "jf": dt("jf", [128, 128], F32), "masks": dt("masks", [128, 2, 128], BF16)}
    io["yT"] = nc.dram_tensor("yT", [256, T2 - 256], BF16, kind="ExternalOutput").ap()
    return io


def build_p2(nc, p, io, ngroups=65, stage="all"):
    NG = ngroups
    NT = 2 * NG
    TT = 128 * NT
    assert TT % SEG == 0 or stage == "a" or True
    nseg = (TT + SEG - 1) // SEG
    featd = nc.dram_tensor("p2_feat", [128, NT2, 2, 128], BF16).ap()
    tokd = nc.dram_tensor("p2_tok", [T2, TOKW], BF16).ap()
    rowd = nc.dram_tensor("p2_row", [4, T2], F32).ap()
    cold = nc.dram_tensor("p2_col", [2, NT2, 128, 4], F32).ap()
    decd = nc.dram_tensor("p2_dec", [2, 128, 2 * NT2], F32).ap()
    hfw = nc.dram_tensor("p2_hfw", [T2, 256], F32).ap()

    idt = nc.alloc_sbuf_tensor("idt", [128, 128], BF16)
    idf = nc.alloc_sbuf_tensor("idf", [128, 128], F32)
    jf = nc.alloc_sbuf_tensor("jfs", [128, 128], F32)
    msk = nc.alloc_sbuf_tensor("msk", [128, 2, 128], BF16)
    gnt = nc.alloc_sbuf_tensor("gnt", [128, 256], F32)
    bgt = nc.alloc_sbuf_tensor("bgt", [128, 4], F32)
    cwt = nc.alloc_sbuf_tensor("cwt", [128, 8], F32)
    epsb = nc.alloc_sbuf_tensor("epsb", [128, 1], F32)
    oneb = nc.alloc_sbuf_tensor("oneb", [128, 1], F32)
    onerow = nc.alloc_sbuf_tensor("onerow", [1, SEG], F32)
    wbf = nc.alloc_sbuf_tensor("wbf", [128, 8, WC2], BF16)
    for (t, src, k) in ((idt, "ident", "idt"), (idf, "identf", "idf"), (jf, "jf", "jf"), (msk, "masks", "msk"), (gnt, "gn", "gnt"), (bgt, "bg", "bgt"), (cwt, "cwq", "cwt")):
        p.dma("sp", t[:], io[src], writes=[k])
    p.op("dve", lambda E: E.memset(epsb[:], 1e-6), writes=["epsb"])
    p.op("dve", lambda E: E.memset(oneb[:], 1.0), writes=["oneb"])
    p.op("dve", lambda E: E.memset(onerow[:], 1.0), writes=["onerow"])
    with ExitStack() as es:
        wst = [es.enter_context(nc.sbuf_tensor("wst%d" % i, [128, WC2], F32)) for i in range(2)]
        wv = io["win"].rearrange("(j q) n -> q j n", q=128)
        for j in range(8):
            s = j % 2
            p.dma("sp", wst[s][:], wv[:, j, :], writes=[("wst", s)])
            p.op("act", lambda E, s=s, j=j: E.activation(out=wbf[:, j, :], in_=wst[s][:], func=AF.Copy), reads=[("wst", s)], writes=["wbf"])
        p.barrier()

    def bpos(tau):
        return 256 - (tau + 1) * 128 if tau < 2 else TT - (tau - 1) * 128

    with ExitStack() as es:
        SB = lambda name, shape, dt: es.enter_context(nc.sbuf_tensor(name + "_a", shape, dt))
        PS = lambda name, dt=F32: es.enter_context(nc.psum_tensor(name + "_a", [128, 512] if dt == F32 else [128, 1024], dt))
        hg = [SB("hg%d" % i, [128, 8, 258], BF16) for i in range(2)]
        cq = SB("cq", [128, 2, 256], F32)
        eq = SB("eq", [128, 2, 256], F32)
        fq = [SB("fq%d" % i, [128, 2, 2, 128], BF16) for i in range(2)]
        tst = [SB("tst%d" % i, [128, TOKW], BF16) for i in range(2)]
        sg = SB("sg", [128, 256], F32)
        gsb = [SB("gsb%d" % i, [128, 8], F32) for i in range(2)]
        rowst = [SB("rowst%d" % i, [1, 4, 256], F32) for i in range(2)]
        pq = PS("pq"); pk = PS("pk"); ptv = PS("ptv"); ptg = PS("ptg"); ptk = PS("ptk", BF16); prow = PS("prow")
        for i in range(2):
            p.op("dve", lambda E, i=i: E.memset(tst[i][:, 256:257], 1.0), writes=[("tst", i)])
        tcount = 0
        for g in range(NG):
            s = g % 2
            t0 = 256 * g
            lo_zero = g in (0, 1)
            hi_zero = g in (0, NG - 1)
            c_lo = 1 if lo_zero else 0
            c_hi = 257 if hi_zero else 258
            p.dma("sp", hg[s][:, :, c_lo:c_hi], io["hT"][:, :, t0 - 1 + c_lo:t0 - 1 + c_hi], writes=[("hg", s)])
            if lo_zero:
                p.op("dve", lambda E, s=s: E.memset(hg[s][:, :, 0:1], 0.0), writes=[("hg", s)])
            if hi_zero:
                p.op("dve", lambda E, s=s: E.memset(hg[s][:, :, 257:258], 0.0), writes=[("hg", s)])
            for (pt, key, c0) in ((pq, "pq", 0), (pk, "pk", 128)):
                for j in range(8):
                    p.op("pe", lambda E, pt=pt, c0=c0, j=j, s=s: E.matmul(pt[:, 0:258], lhsT=wbf[:, j, c0:c0 + 128], rhs=hg[s][:, j, :], start=(j == 0), stop=(j == 7)),
                         reads=["wbf", ("hg", s)], writes=[key])
            for qi, (pt, key) in enumerate(((pq, "pq"), (pk, "pk"))):
                o = 4 * qi
                p.op("dve", lambda E, qi=qi, pt=pt, o=o: E.tensor_scalar(out=cq[:, qi, :], in0=pt[:, 1:257], scalar1=cwt[:, o + 1:o + 2], scalar2=cwt[:, o + 3:o + 4], op0=ALU.mult, op1=ALU.add),
                     reads=["cwt"], writes=["cq", key])
                p.op("dve", lambda E, qi=qi, pt=pt, o=o: E.scalar_tensor_tensor(out=cq[:, qi, :], in0=pt[:, 0:256], scalar=cwt[:, o:o + 1], in1=cq[:, qi, :], op0=ALU.mult, op1=ALU.add),
                     reads=["cwt"], writes=["cq", key])
                p.op("dve", lambda E, qi=qi, pt=pt, o=o: E.scalar_tensor_tensor(out=cq[:, qi, :], in0=pt[:, 2:258], scalar=cwt[:, o + 2:o + 3], in1=cq[:, qi, :], op0=ALU.mult, op1=ALU.add),
                     reads=["cwt"], writes=["cq", key])
            p.op("act", lambda E: E.activation(out=eq[:], in_=cq[:], func=AF.Exp, scale=-1.0), reads=["cq"], writes=["eq"])
            p.op("dve", lambda E: E.tensor_scalar_add(out=eq[:], in0=eq[:], scalar1=1.0), writes=["eq"])
            p.op("dve", lambda E: E.reciprocal(out=eq[:], in_=eq[:]), writes=["eq"])
            for t in range(2):
                ts = slice(t * 128, (t + 1) * 128)
                p.op("dve", lambda E, t=t, ts=ts, s=s: E.tensor_tensor(out=fq[s][:, t, 0, :], in0=cq[:, 0, ts], in1=eq[:, 0, ts], op=ALU.mult), reads=["cq", "eq"], writes=[("fq", s)])
                p.op("dve", lambda E, t=t, ts=ts, s=s: E.scalar_tensor_tensor(out=fq[s][:, t, 1, :], in0=cq[:, 1, ts], scalar=float(128 ** -0.5), in1=eq[:, 1, ts], op0=ALU.mult, op1=ALU.mult),
                     reads=["cq", "eq"], writes=[("fq", s)])
            for t in range(2):
                tau = 2 * g + t
                u = tcount % 2
                tcount += 1
                hs = slice(1 + t * 128, 1 + (t + 1) * 128)
                for (pt, key, c0, N) in ((ptv, "ptv", 256, 512), (ptg, "ptg", 768, 32)):
                    for j in range(8):
                        p.op("pe", lambda E, pt=pt, c0=c0, N=N, j=j, s=s, hs=hs: E.matmul(pt[:, 0:N], lhsT=hg[s][:, j, hs], rhs=wbf[:, j, c0:c0 + N], start=(j == 0), stop=(j == 7)),
                             reads=["wbf", ("hg", s)], writes=[key])
                p.op("pe", lambda E, s=s, t=t: E.transpose(out=ptk[:, 0:128], in_=fq[s][:, t, 1, :], identity=idt[:]), reads=[("fq", s), "idt"], writes=["ptk"])
                p.op("act", lambda E, u=u: E.activation(out=tst[u][:, 0:256], in_=ptv[:, 0:256], func=AF.Copy), writes=[("tst", u), "ptv"])
                p.op("act", lambda E, u=u: E.activation(out=tst[u][:, 257:385], in_=ptk[:, 0:128], func=AF.Copy), writes=[("tst", u), "ptk"])
                p.op("act", lambda E: E.activation(out=sg[:], in_=ptv[:, 256:512], func=AF.Exp, scale=-1.0), writes=["sg", "ptv"])
                p.op("dve", lambda E: E.tensor_scalar_add(out=sg[:], in0=sg[:], scalar1=1.0), writes=["sg"])
                p.op("dve", lambda E: E.reciprocal(out=sg[:], in_=sg[:]), writes=["sg"])
                p.op("act", lambda E, u=u: E.activation(out=tst[u][:, 385:641], in_=sg[:], func=AF.Copy), reads=["sg"], writes=[("tst", u)])
                p.op("dve", lambda E, u=u: E.tensor_tensor(out=gsb[u][:, 0:4], in0=ptg[:, 0:4], in1=bgt[:], op=ALU.add), reads=["bgt"], writes=[("gsb", u), "ptg"])
                p.op("act", lambda E, u=u: E.activation(out=gsb[u][:, 4:6], in_=gsb[u][:, 2:4], func=AF.Exp, scale=-1.0), writes=[("gsb", u)])
                p.op("act", lambda E, u=u: E.activation(out=gsb[u][:, 2:4], in_=gsb[u][:, 4:6], func=AF.Ln, bias=oneb[:, 0:1]), reads=["oneb"], writes=[("gsb", u)])
                for q in range(4):
                    rev = q in (1, 3)
                    p.op("pe", lambda E, q=q, u=u, rev=rev: E.matmul(prow[0:1, q * 128:(q + 1) * 128], lhsT=gsb[u][:, q:q + 1], rhs=(jf if rev else idf)[:], start=True, stop=True),
                         reads=[("gsb", u), "idf", "jf"], writes=["prow"])
                p.op("act", lambda E, s=s, t=t: E.activation(out=rowst[s][0:1, 0:3:2, t * 128:(t + 1) * 128], in_=prow[0:1, 0:512].rearrange("q (a b) -> q a b", a=4)[:, 0:3:2, :], func=AF.Copy),
                     writes=[("rowst", s), "prow"])
                p.op("act", lambda E, s=s, t=t: E.activation(out=rowst[s][0:1, 1:4:2, (1 - t) * 128:(2 - t) * 128], in_=prow[0:1, 0:512].rearrange("q (a b) -> q a b", a=4)[:, 1:4:2, :], func=AF.Copy),
                     writes=[("rowst", s), "prow"])
                p.dma("pool", tokd[tau * 128:(tau + 1) * 128, :], tst[u][:], reads=[("tst", u)], writes=[("tokd", tau)])
            p.dma("pool", featd[:, 2 * g:2 * g + 2, :, :], fq[s][:], reads=[("fq", s)], writes=[("featd", g)])
            pf = 256 * g
            pb = bpos(2 * g + 1)
            for q in range(4):
                pos = pb if q in (1, 3) else pf
                p.dma("pool", rowd[q:q + 1, pos:pos + 256], rowst[s][0:1, q, :], reads=[("rowst", s)], writes=["rowd"])
        p.barrier()
    if stage == "a":
        return

    with ExitStack() as es:
        SB = lambda name, shape, dt: es.enter_context(nc.sbuf_tensor(name + "_s", shape, dt))
        PS = lambda name: es.enter_context(nc.psum_tensor(name + "_s", [128, 512], F32))
        NCH = SEG // 64
        ri = [SB("ri%d" % i, [1, SEG], F32) for i in range(2)]
        rn = [SB("rn%d" % i, [1, SEG], F32) for i in range(2)]
        G = [SB("G%d" % i, [1, SEG], F32) for i in range(2)]
        Bv = SB("Bv", [1, SEG], F32)
        mu = [SB("mu%d" % i, [1, NCH, 64], F32) for i in range(2)]
        cc = SB("cc", [1, NCH, 64], F32)
        ml = SB("ml", [1, NCH, 64], F32)
        Q = [SB("Q%d" % i, [1, 5, SEG], F32) for i in range(2)]
        colst = [SB("colst%d" % i, [128, 40], F32) for i in range(2)]
        colst2 = [SB("colstb%d" % i, [128, 40], F32) for i in range(2)]
        decst = [SB("decst%d" % i, [128, NCH], F32) for i in range(2)]
        pcol = PS("pcol"); pcol2 = PS("pcol2"); pdec = PS("pdec")
        sc = 0
        for d in range(2):
            for sgi in range(nseg):
                s = sc % 2
                sp_ = (sc + 1) % 2
                first = (sgi == 0)
                sc += 1
                p0 = sgi * SEG
                p.dma("sp", ri[s][:], rowd[d:d + 1, p0:p0 + SEG], reads=["rowd"], writes=[("ri", s)])
                p.dma("sp", rn[s][:], rowd[2 + d:3 + d, p0:p0 + SEG], reads=["rowd"], writes=[("rn", s)])
                initG = 0.0 if first else G[sp_][0:1, SEG - 1:SEG]
                initM = 0.0 if first else mu[sp_][0:1, NCH - 1, 63:64]
                p.op("dve", lambda E, s=s, initG=initG: E.tensor_tensor_scan(out=G[s][:], data0=onerow[:], data1=rn[s][:], initial=initG, op0=ALU.mult, op1=ALU.add),
                     reads=[("rn", s), "onerow", ("G", sp_)], writes=[("G", s)])
                p.op("dve", lambda E, s=s: E.tensor_tensor(out=Bv[:], in0=ri[s][:], in1=G[s][:], op=ALU.add), reads=[("ri", s), ("G", s)], writes=["Bv"])
                muf = mu[s][:].rearrange("q a b -> q (a b)")
                p.op("dve", lambda E, s=s, initM=initM, muf=muf: E.tensor_tensor_scan(out=muf, data0=Bv[:], data1=Bv[:], initial=initM, op0=ALU.max, op1=ALU.max),
                     reads=["Bv", ("mu", sp_)], writes=[("mu", s)])
                if first:
                    p.op("dve", lambda E: E.memset(cc[:, 0, :], 0.0), writes=["cc"])
                else:
                    p.op("dve", lambda E, sp_=sp_: E.tensor_copy(out=cc[:, 0, :], in_=mu[sp_][0:1, NCH - 1, 63:64].broadcast_to([1, 64])), reads=[("mu", sp_)], writes=["cc"])
                p.op("dve", lambda E, s=s: E.tensor_copy(out=cc[:, 1:NCH, :], in_=mu[s][0:1, 0:NCH - 1, 63:64].broadcast_to([1, NCH - 1, 64])), reads=[("mu", s)], writes=["cc"])
                p.op("dve", lambda E, s=s: E.tensor_copy(out=ml[:], in_=mu[s][0:1, :, 63:64].broadcast_to([1, NCH, 64])), reads=[("mu", s)], writes=["ml"])
                ccf = cc[:].rearrange("q a b -> q (a b)")
                mlf = ml[:].rearrange("q a b -> q (a b)")
                p.op("dve", lambda E, s=s, ccf=ccf: E.tensor_tensor(out=Q[s][0:1, 0, :], in0=Bv[:], in1=ccf, op=ALU.subtract), reads=["Bv", "cc"], writes=[("Q", s)])
                p.op("dve", lambda E, s=s, ccf=ccf, muf=muf: E.tensor_tensor(out=Q[s][0:1, 1, :], in0=ccf, in1=muf, op=ALU.subtract), reads=["cc", ("mu", s)], writes=[("Q", s)])
                p.op("dve", lambda E, s=s, muf=muf: E.tensor_tensor(out=Q[s][0:1, 2, :], in0=G[s][:], in1=muf, op=ALU.subtract), reads=[("G", s), ("mu", s)], writes=[("Q", s)])
                p.op("dve", lambda E, s=s, mlf=mlf: E.tensor_tensor(out=Q[s][0:1, 3, :], in0=Bv[:], in1=mlf, op=ALU.subtract), reads=["Bv", "ml"], writes=[("Q", s)])
                p.op("dve", lambda E, s=s, ccf=ccf, mlf=mlf: E.tensor_tensor(out=Q[s][0:1, 4, :], in0=ccf, in1=mlf, op=ALU.subtract), reads=["cc", "ml"], writes=[("Q", s)])
                p.op("act", lambda E, s=s: E.activation(out=Q[s][:], in_=Q[s][:], func=AF.Exp), writes=[("Q", s)])
                ntl = SEG // 128
                for tl_ in range(ntl):
                    for q in range(4):
                        p.op("pe", lambda E, s=s, tl_=tl_, q=q: E.matmul(pcol[:, tl_ * 4 + q:tl_ * 4 + q + 1], lhsT=Q[s][0:1, q, tl_ * 128:(tl_ + 1) * 128], rhs=onerow[0:1, 0:1], start=True, stop=True),
                             reads=[("Q", s), "onerow"], writes=["pcol"])
                p.op("dve", lambda E, s=s: E.tensor_copy(out=colst[s][:], in_=pcol[:, 0:40]), writes=[("colst", s), "pcol"])
                src = colst[s]
                skey = ("colst", s)
                if d == 1:
                    p.op("pe", lambda E, s=s: E.matmul(pcol2[:, 0:40], lhsT=jf[:], rhs=colst[s][:], start=True, stop=True), reads=[("colst", s), "jf"], writes=["pcol2"])
                    p.op("dve", lambda E, s=s: E.tensor_copy(out=colst2[s][:], in_=pcol2[:, 0:40]), writes=[("colst2", s), "pcol2"])
                    src = colst2[s]
                    skey = ("colst2", s)
                for tl_ in range(ntl):
                    pbk = sgi * ntl + tl_
                    if d == 0:
                        tau = pbk
                    else:
                        tau = (1 - pbk) if pbk < 2 else (NT + 1 - pbk)
                    p.dma("pool", cold[d, tau, :, :], src[:, tl_ * 4:tl_ * 4 + 4], reads=[skey], writes=[("cold", d, tau)])
                p.op("pe", lambda E, s=s: E.matmul(pdec[:, 0:NCH], lhsT=onerow[0:1, 0:128], rhs=Q[s][0:1, 4, 0:SEG:64], start=True, stop=True), reads=[("Q", s), "onerow"], writes=["pdec"])
                p.op("dve", lambda E, s=s: E.tensor_copy(out=decst[s][:], in_=pdec[:, 0:NCH]), writes=[("decst", s), "pdec"])
                p.dma("pool", decd[d, :, sgi * NCH:(sgi + 1) * NCH], decst[s][:], reads=[("decst", s)], writes=[("decd", d)])
        p.barrier()
    if stage == "s":
        return

    lat_tiles = list(range(2, NT))
    order_f = [0, 1] + lat_tiles
    order_r = [1, 0] + lat_tiles[::-1]
    with ExitStack() as es:
        SB = lambda name, shape, dt: es.enter_context(nc.sbuf_tensor(name + "_c", shape, dt))
        PS = lambda name, dt=F32: es.enter_context(nc.psum_tensor(name + "_c", [128, 512] if dt == F32 else [128, 1024], dt))
        fl = [SB("fl%d" % i, [128, 2, 128], BF16) for i in range(3)]
        tl = [SB("tl%d" % i, [128, TOKW], BF16) for i in range(3)]
        cl = [SB("cl%d" % i, [128, 4], F32) for i in range(3)]
        dl = [SB("dl%d" % i, [128, 2], F32) for i in range(3)]
        ol = [SB("ol%d" % i, [128, 256], F32) for i in range(3)]
        sTm = [SB("sTm%d" % i, [128, 128], BF16) for i in range(2)]
        kw = [SB("kw%d" % i, [128, 128], BF16) for i in range(2)]
        C32 = SB("C32", [128, 257], F32)
        Cbf = [SB("Cbf%d" % i, [128, 257], BF16) for i in range(3)]
        ost = [SB("ost%d" % i, [128, 256], F32) for i in range(2)]
        sm = [SB("sm%d" % i, [128, 4], F32) for i in range(2)]
        osq = SB("osq", [128, 256], BF16)
        rs = [SB("rs%d" % i, [128, 4], F32) for i in range(2)]
        yb = [SB("yb%d" % i, [128, 256], BF16) for i in range(2)]
        yts = [SB("yts%d" % i, [128, 2, 128], BF16) for i in range(2)]
        psc = [PS("psc%d" % i) for i in range(2)]
        po = [PS("po%d" % i) for i in range(2)]
        pS = [PS("pS%d" % i) for i in range(2)]
        pyT = PS("pyT", BF16)
        yTv = io["yT"].rearrange("(c q) t -> q c t", q=128)
        for dirn, order in ((0, order_f), (1, order_r))[:(1 if stage == "b" else 2)]:
            p.op("dve", lambda E: E.memset(C32[:], 0.0), writes=["C32"])
            sb = 0
            p.op("dve", lambda E, sb=sb: E.memset(Cbf[sb][:], 0.0), writes=[("Cbf", sb)])
            chunk_order = (0, 1) if dirn == 0 else (1, 0)
            for n, tg in enumerate(order):
                s3 = n % 3
                s2 = n % 2
                is_lat = tg >= 2
                p.dma("sp", fl[s3][:], featd[:, tg, :, :], reads=[("featd", tg // 2)], writes=[("fl", s3)])
                p.dma("sp", tl[s3][:], tokd[tg * 128:(tg + 1) * 128, :], reads=[("tokd", tg)], writes=[("tl", s3)])
                p.dma("sp", cl[s3][:], cold[dirn, tg, :, :], reads=[("cold", dirn, tg)], writes=[("cl", s3)])
                p.dma("sp", dl[s3][:], decd[dirn, :, 2 * n:2 * n + 2], reads=[("decd", dirn)], writes=[("dl", s3)])
                if dirn == 1 and is_lat:
                    p.dma("sp", ol[s3][:], hfw[tg * 128:(tg + 1) * 128, :], reads=[("hfw", tg)], writes=[("ol", s3)])
                p.op("pe", lambda E, s2=s2, s3=s3: E.matmul(psc[s2][:, 0:128], lhsT=fl[s3][:, 1, :], rhs=fl[s3][:, 0, :], start=True, stop=True),
                     reads=[("fl", s3)], writes=[("psc", s2)])
                p.op("dve", lambda E, s2=s2, s3=s3, dirn=dirn: E.scalar_tensor_tensor(out=sTm[s2][:], in0=psc[s2][:, 0:128], scalar=cl[s3][:, 0:1], in1=msk[:, dirn, :], op0=ALU.mult, op1=ALU.mult),
                     reads=["msk", ("cl", s3)], writes=[("sTm", s2), ("psc", s2)])
                p.op("dve", lambda E, s2=s2, s3=s3: E.tensor_scalar(out=kw[s2][:], in0=tl[s3][:, 257:385], scalar1=cl[s3][:, 3:4], scalar2=None, op0=ALU.mult),
                     reads=[("tl", s3), ("cl", s3)], writes=[("kw", s2)])
                p.op("pe", lambda E, s2=s2, s3=s3: E.matmul(po[s2][:, 0:257], lhsT=sTm[s2][:], rhs=tl[s3][:, 0:257], start=True, stop=False),
                     reads=[("sTm", s2), ("tl", s3)], writes=[("po", s2)])
                for ci, c in enumerate(chunk_order):
                    cs = slice(64 * c, 64 * (c + 1))
                    p.op("pe", lambda E, s2=s2, s3=s3, cs=cs, sb=sb: E.matmul(po[s2][cs, 0:257], lhsT=fl[s3][:, 0, cs], rhs=Cbf[sb][:], start=False, stop=True),
                         reads=[("fl", s3), ("Cbf", sb)], writes=[("po", s2)])
                    pi = (2 * n + ci) % 2
                    p.op("pe", lambda E, pi=pi, s2=s2, s3=s3, cs=cs: E.matmul(pS[pi][:, 0:257], lhsT=kw[s2][cs, :], rhs=tl[s3][cs, 0:257], start=True, stop=True),
                         reads=[("kw", s2), ("tl", s3)], writes=[("pS", pi)])
                    p.op("dve", lambda E, pi=pi, s3=s3, ci=ci: E.scalar_tensor_tensor(out=C32[:], in0=C32[:], scalar=dl[s3][:, ci:ci + 1], in1=pS[pi][:, 0:257], op0=ALU.mult, op1=ALU.add),
                         reads=[("dl", s3)], writes=["C32", ("pS", pi)])
                    sb = (sb + 1) % 3
                    p.op("act", lambda E, sb=sb: E.activation(out=Cbf[sb][:], in_=C32[:], func=AF.Copy), reads=["C32"], writes=[("Cbf", sb)])
                if not is_lat:
                    continue
                p.op("act", lambda E, s2=s2, s3=s3: E.activation(out=sm[s2][:, 0:1], in_=po[s2][:, 256:257], func=AF.Abs, scale=cl[s3][:, 1:2]),
                     reads=[("cl", s3)], writes=[("sm", s2), ("po", s2)])
                p.op("dve", lambda E, s2=s2, s3=s3: E.tensor_tensor(out=sm[s2][:, 1:2], in0=sm[s2][:, 0:1], in1=cl[s3][:, 2:3], op=ALU.max), reads=[("cl", s3)], writes=[("sm", s2)])
                p.op("dve", lambda E, s2=s2: E.reciprocal(out=sm[s2][:, 2:3], in_=sm[s2][:, 1:2]), writes=[("sm", s2)])
                p.op("dve", lambda E, s2=s2, s3=s3: E.tensor_tensor(out=sm[s2][:, 3:4], in0=sm[s2][:, 2:3], in1=cl[s3][:, 1:2], op=ALU.mult), reads=[("cl", s3)], writes=[("sm", s2)])
                if dirn == 0:
                    p.op("act", lambda E, s2=s2: E.activation(out=ost[s2][:], in_=po[s2][:, 0:256], func=AF.Copy, scale=sm[s2][:, 3:4]), reads=[("sm", s2)], writes=[("ost", s2), ("po", s2)])
                    p.dma("pool", hfw[tg * 128:(tg + 1) * 128, :], ost[s2][:], reads=[("ost", s2)], writes=[("hfw", tg)])
                else:
                    p.op("dve", lambda E, s2=s2, s3=s3: E.scalar_tensor_tensor(out=ost[s2][:], in0=po[s2][:, 0:256], scalar=sm[s2][:, 3:4], in1=ol[s3][:], op0=ALU.mult, op1=ALU.add),
                         reads=[("sm", s2), ("ol", s3)], writes=[("ost", s2), ("po", s2)])
                    p.op("act", lambda E, s2=s2: E.activation(out=osq[:], in_=ost[s2][:], func=AF.Square, accum_out=rs[s2][:, 0:1]), reads=[("ost", s2)], writes=["osq", ("rs", s2)])
                    p.op("act", lambda E, s2=s2: E.activation(out=rs[s2][:, 1:2], in_=rs[s2][:, 0:1], func=AF.Ln, scale=1.0 / 256, bias=epsb[:, 0:1]), reads=["epsb"], writes=[("rs", s2)])
                    p.op("act", lambda E, s2=s2: E.activation(out=rs[s2][:, 2:3], in_=rs[s2][:, 1:2], func=AF.Exp, scale=-0.5), writes=[("rs", s2)])
                    p.op("dve", lambda E, s2=s2: E.scalar_tensor_tensor(out=ost[s2][:], in0=ost[s2][:], scalar=rs[s2][:, 2:3], in1=gnt[:], op0=ALU.mult, op1=ALU.mult),
                         reads=[("rs", s2), "gnt"], writes=[("ost", s2)])
                    p.op("dve", lambda E, s2=s2, s3=s3: E.tensor_tensor(out=yb[s2][:], in0=ost[s2][:], in1=tl[s3][:, 385:641], op=ALU.mult), reads=[("ost", s2), ("tl", s3)], writes=[("yb", s2)])
                    for c in range(2):
                        p.op("pe", lambda E, s2=s2, c=c: E.transpose(out=pyT[:, c * 128:(c + 1) * 128], in_=yb[s2][:, c * 128:(c + 1) * 128], identity=idt[:]), reads=[("yb", s2), "idt"], writes=["pyT"])
                    p.op("act", lambda E, s2=s2: E.activation(out=yts[s2][:], in_=pyT[:, 0:256].rearrange("q (a b) -> q a b", a=2), func=AF.Copy), writes=[("yts", s2), "pyT"])
                    p.dma("pool", yTv[:, :, (tg - 2) * 128:(tg - 1) * 128], yts[s2][:], reads=[("yts", s2)])
        p.barrier()


import numpy as np
import ml_dtypes
BF = ml_dtypes.bfloat16

def p13_inputs(inp, core, layer, yT_heads, xfull, ctxfull=None):
    b, q = core // 4, core % 4
    has_ctx = ctxfull is not None
    f = np.float32
    r0 = 64 * q - 1
    rows = np.arange(r0, r0 + 66)
    valid = (rows >= 0) & (rows < 256)
    tok = (rows[:, None] * 64 + np.arange(64)[None, :])
    tokc = np.clip(tok, 0, 16383).reshape(-1)
    vmask = np.repeat(valid, 64)
    off = 256 if has_ctx else 0
    NY = 4224 + off
    yT4 = np.zeros((4, 256, NY), BF)
    for h in range(4):
        src = yT_heads[b * 4 + h]
        lat = src[:, off + tokc].copy()
        lat[:, ~vmask] = 0
        yT4[h, :, 0:4224] = lat
        if has_ctx:
            yT4[h, :, 4224:] = src[:, 0:256]
    xin = np.zeros((4224 + off, 1024), f)
    xl = xfull[b][tokc].copy(); xl[~vmask] = 0
    xin[0:4224] = xl
    if has_ctx:
        xin[4224:] = ctxfull[b]
    cT = np.stack([inp["c"][b], inp["c_ctx"]], axis=-1).reshape(8, 128, 2).transpose(1, 0, 2)
    if has_ctx:
        adaw = np.concatenate([inp["ada_w"][layer][:, 2048:6144], inp["ada_w"][layer + 1][:, 0:2048]], axis=1)
        adab = np.concatenate([inp["ada_b"][layer][2048:6144], inp["ada_b"][layer + 1][0:2048]])
    else:
        adaw = inp["ada_w"][layer][:, 2048:6144]
        adab = inp["ada_b"][layer][2048:6144]
    NM = adaw.shape[1] // 128
    d = {"yT4": yT4, "xin": xin, "cT": np.ascontiguousarray(cT), "adaw": np.ascontiguousarray(adaw),
         "adabT": np.ascontiguousarray(adab.reshape(NM, 128).T), "gffnT": np.ascontiguousarray(inp["norm_ffn_g"][layer].reshape(8, 128).T),
         "wout": inp["gla_w_out"][0] if layer == 0 else inp["mlstm_w_out"][0],
         "wup": inp["ffn_w_up"][layer], "wdn": inp["ffn_w_down"][layer],
         "convw": np.ascontiguousarray(inp["ffn_conv_w"][layer].reshape(9, 22, 128).transpose(2, 1, 0)),
         "convb": np.ascontiguousarray(inp["ffn_conv_b"][layer].reshape(22, 128).T),
         "hv": np.ascontiguousarray(np.broadcast_to(np.array([float(valid[0]), float(valid[65])], f)[None, :], (128, 2))),
         "ident": np.eye(128, dtype=f).astype(BF), "identf": np.eye(128, dtype=f)}
    if has_ctx:
        d["gnextT"] = np.ascontiguousarray(inp["norm_mix_g"][layer + 1].reshape(8, 128).T)
    else:
        d["fgbc"] = np.ascontiguousarray(np.broadcast_to(inp["final_norm_g"][None, :], (128, 1024)))
    return d

def p13_decl(nc, has_ctx, F32, BF16):
    dt = lambda n, s, t: nc.dram_tensor(n, s, t, kind="ExternalInput").ap()
    off = 256 if has_ctx else 0
    NM = 48 if has_ctx else 32
    io = {"yT4": dt("yT4", [4, 256, 4224 + off], BF16), "xin": dt("xin", [4224 + off, 1024], F32), "cT": dt("cT", [128, 8, 2], F32),
          "adaw": dt("adaw", [1024, NM * 128], F32), "adabT": dt("adabT", [128, NM], F32), "gffnT": dt("gffnT", [128, 8], F32),
          "wout": dt("wout", [1024, 1024], F32), "wup": dt("wup", [1024, 5632], F32), "wdn": dt("wdn", [2816, 1024], F32),
          "convw": dt("convw", [128, 22, 9], F32), "convb": dt("convb", [128, 22], F32), "hv": dt("hv", [128, 2], F32),
          "ident": dt("ident", [128, 128], BF16), "identf": dt("identf", [128, 128], F32)}
    if has_ctx:
        io["gnextT"] = dt("gnextT", [128, 8], F32)
        io["x2"] = nc.dram_tensor("x2", [4096, 1024], F32, kind="ExternalOutput").ap()
        io["hnT"] = nc.dram_tensor("hnT", [128, 8, 4096 + 256], BF16, kind="ExternalOutput").ap()
    else:
        io["fgbc"] = dt("fgbc", [128, 1024], F32)
        io["out"] = nc.dram_tensor("out", [4096, 1024], F32, kind="ExternalOutput").ap()
    return io


def _run(nc, in_maps):
    res = run_bass_kernel_spmd(nc, in_maps, core_ids=list(range(8)))
    return res.results


def _build_p0():
    nc = bass.Bass("TRN2", target_bir_lowering=False)
    dt = lambda n, s, t: nc.dram_tensor(n, s, t, kind="ExternalInput").ap()
    xall = dt("xall", [T_ALL, 1024], F32); cT = dt("cT", [128, 8, 2], F32); adaw = dt("adaw", [1024, 2048], F32)
    adabT = dt("adabT", [128, 16], F32); gmixT = dt("gmixT", [128, 8], F32); win = dt("win", [1024, WCOLS], F32)
    wa2 = dt("wa2", [64, 256], F32); ba2 = dt("ba2", [128, 256], F32); gn = dt("gn", [128, 256], F32); ident = dt("ident", [128, 128], BF16)
    masks = dt("masks", [128, 8, 128], BF16)
    yT = nc.dram_tensor("yT", [256, T_ALL], BF16, kind="ExternalOutput").ap()
    p = Prog(nc)
    build_p0(nc, p, xall, cT, adaw, adabT, gmixT, win, wa2, ba2, gn, ident, masks, yT)
    p.finish()
    return nc


def _build_p13(has_ctx, final):
    nc = bass.Bass("TRN2", target_bir_lowering=False)
    io = p13_decl(nc, has_ctx, F32, BF16)
    p = Prog(nc)
    build_p13(nc, p, io, has_ctx=has_ctx, final=final)
    p.finish()
    return nc


def _build_p2():
    nc = bass.Bass("TRN2", target_bir_lowering=False)
    io = p2_decl(nc)
    p = Prog(nc)
    build_p2(nc, p, io)
    p.finish()
    return nc


def kernel(**inputs):
    inp = {k: np.asarray(v) for k, v in inputs.items()}
    r0 = _run(_build_p0(), [p0_inputs(inp, c) for c in range(8)])
    yT0 = [np.asarray(r["yT"]) for r in r0]
    r1 = _run(_build_p13(True, False), [p13_inputs(inp, c, 0, yT0, inp["x"], inp["ctx"]) for c in range(8)])
    x1 = np.stack([np.concatenate([np.asarray(r1[b * 4 + q]["x2"]) for q in range(4)], axis=0) for b in range(2)])
    hT_b = []
    for b in range(2):
        lat = np.concatenate([np.asarray(r1[b * 4 + q]["hnT"])[:, :, 0:4096] for q in range(4)], axis=2)
        lat = lat.reshape(128, 8, 256, 64).transpose(0, 1, 3, 2).reshape(128, 8, 16384)
        hT_b.append(np.ascontiguousarray(np.concatenate([np.asarray(r1[b * 4]["hnT"])[:, :, 4096:4352], lat], axis=2)))
    r2 = _run(_build_p2(), [p2_inputs(inp, c, hT_b[c // 4]) for c in range(8)])
    yT1 = [np.ascontiguousarray(np.asarray(r["yT"]).reshape(256, 64, 256).transpose(0, 2, 1).reshape(256, 16384)) for r in r2]
    r3 = _run(_build_p13(False, True), [p13_inputs(inp, c, 1, yT1, x1, None) for c in range(8)])
    out = np.stack([np.concatenate([np.asarray(r3[b * 4 + q]["out"]) for q in range(4)], axis=0) for b in range(2)])
    return out.astype(np.float32)
```
